# Optimizing a Trainium2 kernel written in Bass

```python
import jax
import jax.numpy as jnp
from jax import lax
import numpy as np

D_MODEL = 2048
BATCH = 8
SEQ = 4096
DEPTH = 2

PLE_DIM = 256
ROPE_THETA = 500000.0
EPS = 1e-6
Q_BLOCK = 128

MLA_HEADS = 6
MLA_Q_LORA = 512
MLA_KV_LORA = 512
MLA_NOPE = 128
MLA_ROPE = 64
MLA_V = 128
MLA_QK = MLA_NOPE + MLA_ROPE
MLA_WIDTH = MLA_HEADS * MLA_V

CONV_CH = 512
CONV_K = 3

DSA_HEADS = 6
DSA_KV_HEADS = 2
DSA_GROUP = DSA_HEADS // DSA_KV_HEADS
DSA_HEAD_DIM = 128
DSA_WIDTH = DSA_HEADS * DSA_HEAD_DIM
DSA_ROT = DSA_HEAD_DIM // 4
IDX_HEADS = 16
IDX_DIM = 64
IDX_ROT = IDX_DIM // 4
TOPK_MAX = 256

MIX_WIDTH = MLA_WIDTH + CONV_CH + DSA_WIDTH

SPLIT_SIZES = (
    MLA_Q_LORA, MLA_KV_LORA, MLA_ROPE, MLA_WIDTH,
    CONV_CH, CONV_CH, CONV_CH, CONV_CH,
    DSA_HEADS * DSA_HEAD_DIM, DSA_KV_HEADS * DSA_HEAD_DIM,
    DSA_KV_HEADS * DSA_HEAD_DIM, DSA_WIDTH,
    IDX_HEADS * IDX_DIM, IDX_HEADS, IDX_DIM,
)
N_IN = sum(SPLIT_SIZES)

kernel_name = "hybrid_mla_conv_dsa_block"


def _split(z, sizes):
    idx = []
    acc = 0
    for s in sizes[:-1]:
        acc += s
        idx.append(acc)
    return jnp.split(z, idx, axis=-1)


def rms_norm(x, g):
    xf = x.astype(jnp.float32)
    y = xf * lax.rsqrt(jnp.mean(xf * xf, axis=-1, keepdims=True) + EPS)
    return (y * g.astype(jnp.float32)).astype(x.dtype)


def rope(x, pos):
    half = x.shape[-1] // 2
    inv = ROPE_THETA ** (-jnp.arange(half, dtype=jnp.float32) / half)
    ang = pos.astype(jnp.float32)[:, :, None, None] * inv
    cos, sin = jnp.cos(ang), jnp.sin(ang)
    xf = x.astype(jnp.float32)
    x1, x2 = xf[..., :half], xf[..., half:]
    return jnp.concatenate([x1 * cos - x2 * sin, x2 * cos + x1 * sin], axis=-1).astype(x.dtype)


def partial_rope(x, pos, rot):
    return jnp.concatenate([rope(x[..., :rot], pos), x[..., rot:]], axis=-1)


def _to_blocks(a):
    b, s = a.shape[:2]
    return jnp.moveaxis(a.reshape((b, s // Q_BLOCK, Q_BLOCK) + a.shape[2:]), 1, 0)


def _from_blocks(a):
    nb, b, q = a.shape[:3]
    return jnp.moveaxis(a, 0, 1).reshape((b, nb * q) + a.shape[3:])


def mla_attention(q, k, v):
    s_len = q.shape[1]
    scale = q.shape[-1] ** -0.5
    kpos = jnp.arange(s_len)

    def block(args):
        qb, start = args
        s = jnp.einsum("bqhd,bkhd->bhqk", qb, k).astype(jnp.float32) * scale
        qpos = start + jnp.arange(Q_BLOCK)
        s = jnp.where(kpos[None, :] <= qpos[:, None], s, -jnp.inf)
        pr = jax.nn.softmax(s, axis=-1).astype(v.dtype)
        return jnp.einsum("bhqk,bkhd->bqhd", pr, v)

    starts = jnp.arange(s_len // Q_BLOCK) * Q_BLOCK
    return _from_blocks(lax.map(block, (_to_blocks(q), starts)))


def dsa_attention(q, k, v, iq, iw, ik, k_top):
    b, s_len = q.shape[:2]
    scale = q.shape[-1] ** -0.5
    kpos = jnp.arange(s_len)
    bidx = jnp.arange(b)[:, None, None]
    ikf = ik.astype(jnp.float32)

    def block(args):
        qb, iqb, iwb, start = args
        qpos = start + jnp.arange(Q_BLOCK)
        causal = kpos[None, :] <= qpos[:, None]
        logits = jnp.einsum("bqhd,bkd->bqhk", iqb.astype(jnp.float32), ikf)
        score = jnp.einsum("bqh,bqhk->bqk", iwb.astype(jnp.float32), jax.nn.relu(logits))
        score = jnp.where(causal[None], score, -jnp.inf)
        _, idx = lax.top_k(score, k_top)
        valid = idx <= qpos[None, :, None]
        ks = k[bidx, idx]
        vs = v[bidx, idx]
        s = jnp.einsum("bqgrd,bqkgd->bqgrk", qb, ks).astype(jnp.float32) * scale
        s = jnp.where(valid[:, :, None, None, :], s, -jnp.inf)
        pr = jax.nn.softmax(s, axis=-1).astype(v.dtype)
        return jnp.einsum("bqgrk,bqkgd->bqgrd", pr, vs)

    starts = jnp.arange(s_len // Q_BLOCK) * Q_BLOCK
    return _from_blocks(lax.map(block, (_to_blocks(q), _to_blocks(iq), _to_blocks(iw), starts)))


def hybrid_layer(h, p_i, positions, norm_in, w_in, mla_gq, mla_w_uq, mla_gkv, mla_w_ukv,
                 mla_qn, mla_kn, conv_w, dsa_qn, dsa_kn, w_out, ple_norm, ple_w_gate, ple_w_proj):
    b, s_len, _ = h.shape
    a = rms_norm(h, norm_in)
    z = a @ w_in
    (cq, ckv, kpe, mla_z, cx, cb, cc, conv_z,
     dq, dk, dv, dsa_z, iq, iw, ik) = _split(z, SPLIT_SIZES)

    q = (rms_norm(cq, mla_gq) @ mla_w_uq).reshape(b, s_len, MLA_HEADS, MLA_QK)
    kv = (rms_norm(ckv, mla_gkv) @ mla_w_ukv).reshape(b, s_len, MLA_HEADS, MLA_NOPE + MLA_V)
    k_nope, v = kv[..., :MLA_NOPE], kv[..., MLA_NOPE:]
    k_pe = jnp.broadcast_to(kpe[:, :, None, :], (b, s_len, MLA_HEADS, MLA_ROPE))
    k = jnp.concatenate([k_nope, k_pe], axis=-1)
    q = rms_norm(q, mla_qn)
    k = rms_norm(k, mla_kn)
    q = jnp.concatenate([q[..., :MLA_NOPE], rope(q[..., MLA_NOPE:], positions)], axis=-1)
    k = jnp.concatenate([k[..., :MLA_NOPE], rope(k[..., MLA_NOPE:], positions)], axis=-1)
    y_a = mla_attention(q, k, v).reshape(b, s_len, MLA_WIDTH) * jax.nn.silu(mla_z)

    u = cc * cx
    up = jnp.pad(u, ((0, 0), (CONV_K - 1, 0), (0, 0)))
    conv = (conv_w[0] * up[:, 0:s_len] + conv_w[1] * up[:, 1:s_len + 1]
            + conv_w[2] * up[:, 2:s_len + 2])
    y_b = cb * conv * jax.nn.silu(conv_z)

    dq = partial_rope(rms_norm(dq.reshape(b, s_len, DSA_HEADS, DSA_HEAD_DIM), dsa_qn), positions, DSA_ROT)
    dq = dq.reshape(b, s_len, DSA_KV_HEADS, DSA_GROUP, DSA_HEAD_DIM)
    dk = partial_rope(rms_norm(dk.reshape(b, s_len, DSA_KV_HEADS, DSA_HEAD_DIM), dsa_kn), positions, DSA_ROT)
    dv = dv.reshape(b, s_len, DSA_KV_HEADS, DSA_HEAD_DIM)
    iq = partial_rope(iq.reshape(b, s_len, IDX_HEADS, IDX_DIM), positions, IDX_ROT) * (IDX_DIM ** -0.5)
    ik = partial_rope(ik[:, :, None, :], positions, IDX_ROT)[:, :, 0, :]
    iw = iw * (IDX_HEADS ** -0.5)
    k_top = min(TOPK_MAX, s_len // 4)
    y_c = dsa_attention(dq, dk, dv, iq, iw, ik, k_top).reshape(b, s_len, DSA_WIDTH) * jax.nn.silu(dsa_z)

    mix = jnp.concatenate([y_a, y_b, y_c], axis=-1)
    h = h + mix @ w_out

    gate = jax.nn.sigmoid(rms_norm(h, ple_norm) @ ple_w_gate)
    return h + gate * (p_i @ ple_w_proj)


def setup_inputs(seed: int = 0) -> dict:
    key = jax.random.key(seed)
    ks = jax.random.split(key, 20)

    def dense(k, shape, fan_in):
        return jax.random.normal(k, shape, jnp.float32) * (fan_in ** -0.5)

    def gain(k, shape):
        return 1.0 + 0.05 * jax.random.normal(k, shape, jnp.float32)

    x = jax.random.normal(ks[0], (BATCH, SEQ, D_MODEL), jnp.float32)
    p = jax.random.normal(ks[1], (DEPTH, BATCH, SEQ, PLE_DIM), jnp.float32)
    offs = jax.random.randint(ks[2], (BATCH, 1), 0, 1024, jnp.int32)
    positions = offs + jnp.arange(SEQ, dtype=jnp.int32)[None, :]
    return {
        "x": x,
        "p": p,
        "positions": positions,
        "norm_in": gain(ks[3], (DEPTH, D_MODEL)),
        "w_in": dense(ks[4], (DEPTH, D_MODEL, N_IN), D_MODEL),
        "mla_gq": gain(ks[5], (DEPTH, MLA_Q_LORA)),
        "mla_w_uq": dense(ks[6], (DEPTH, MLA_Q_LORA, MLA_HEADS * MLA_QK), MLA_Q_LORA),
        "mla_gkv": gain(ks[7], (DEPTH, MLA_KV_LORA)),
        "mla_w_ukv": dense(ks[8], (DEPTH, MLA_KV_LORA, MLA_HEADS * (MLA_NOPE + MLA_V)), MLA_KV_LORA),
        "mla_qn": gain(ks[9], (DEPTH, MLA_QK)),
        "mla_kn": gain(ks[10], (DEPTH, MLA_QK)),
        "conv_w": dense(ks[11], (DEPTH, CONV_K, CONV_CH), CONV_K),
        "dsa_qn": gain(ks[12], (DEPTH, DSA_HEAD_DIM)),
        "dsa_kn": gain(ks[13], (DEPTH, DSA_HEAD_DIM)),
        "w_out": dense(ks[14], (DEPTH, MIX_WIDTH, D_MODEL), MIX_WIDTH),
        "ple_norm": gain(ks[15], (DEPTH, D_MODEL)),
        "ple_w_gate": dense(ks[16], (DEPTH, D_MODEL, D_MODEL), D_MODEL),
        "ple_w_proj": dense(ks[17], (DEPTH, PLE_DIM, D_MODEL), PLE_DIM),
    }


def reference(x, p, positions, norm_in, w_in, mla_gq, mla_w_uq, mla_gkv, mla_w_ukv, mla_qn, mla_kn,
              conv_w, dsa_qn, dsa_kn, w_out, ple_norm, ple_w_gate, ple_w_proj):
    h = x
    for i in range(DEPTH):
        h = hybrid_layer(h, p[i], positions, norm_in[i], w_in[i], mla_gq[i], mla_w_uq[i], mla_gkv[i],
                         mla_w_ukv[i], mla_qn[i], mla_kn[i], conv_w[i], dsa_qn[i], dsa_kn[i], w_out[i],
                         ple_norm[i], ple_w_gate[i], ple_w_proj[i])
    return h
```

```python
import math
from contextlib import ExitStack

import numpy as np
import ml_dtypes
import concourse.bass as bass
import concourse.mybir as mybir
from concourse.bass_utils import run_bass_kernel_spmd

F32 = mybir.dt.float32
BF16 = mybir.dt.bfloat16
I32 = mybir.dt.int32
ALU = mybir.AluOpType
AF = mybir.ActivationFunctionType

D_MODEL = 2048
N_IN = 7056
N_SC = 7120
EPS = 1e-6
TOPK = 256
NBISECT = 20
PI = math.pi


class StopBuild(Exception):
    pass


class R:
    __slots__ = ("name", "w", "r")

    def __init__(self, name=""):
        self.name = name
        self.w = None
        self.r = {}


class Tile:
    def __init__(self, t, name="", nres=1):
        self.t = t
        self.rs = [R("%s.%d" % (name, i)) for i in range(nres)]

    def __getitem__(self, idx):
        return self.t[idx]

    @property
    def r(self):
        return self.rs[0]


class FW:
    ENG = ("pe", "act", "dve", "pool", "sp")

    def __init__(self, nc, es, n_dma_sems=8):
        self.nc = nc
        self.eng = {"pe": nc.tensor, "act": nc.scalar, "dve": nc.vector, "pool": nc.gpsimd, "sp": nc.sync}
        self.sem = {}
        self.cnt = {}
        for e in self.ENG:
            self.sem[e] = es.enter_context(nc.semaphore("s_" + e))
            self.cnt[e] = 0
        self.dkeys = []
        for i in range(n_dma_sems):
            k = "d%d" % i
            self.sem[k] = es.enter_context(nc.semaphore("s_" + k))
            self.cnt[k] = 0
            self.dkeys.append(k)
        self.dma_i = 0
        self.waited = {e: {} for e in self.ENG}
        self.ninstr = 0
        self.dead = False

    def _wait(self, e, deps):
        for k, v in deps.items():
            if v > self.cnt[k]:
                raise RuntimeError("dependency on future instruction %s %d > %d" % (k, v, self.cnt[k]))
            if self.waited[e].get(k, 0) < v:
                self.eng[e].wait_ge(self.sem[k], v)
                self.waited[e][k] = v
                self.ninstr += 1

    @staticmethod
    def _add(deps, kv):
        k, v = kv
        if deps.get(k, 0) < v:
            deps[k] = v

    def _deps(self, e, reads, writes):
        deps = {}
        for r in reads:
            if r.w is not None:
                self._add(deps, r.w)
        for w in writes:
            if w.w is not None and not (e == "pe" and w.w[0] == "pe"):
                self._add(deps, w.w)
            for k, v in w.r.items():
                if not (e == "pe" and k == "pe"):
                    self._add(deps, (k, v))
        return deps

    def op(self, e, fn, reads=(), writes=(), inc=True):
        if self.dead:
            return None
        self._wait(e, self._deps(e, reads, writes))
        ins = fn(self.eng[e])
        self.ninstr += 1
        if inc:
            self.cnt[e] += 1
            ins.then_inc(self.sem[e], 1)
            val = self.cnt[e]
        else:
            val = self.cnt[e] + 1
        for r in reads:
            if r.r.get(e, 0) < val:
                r.r[e] = val
        for w in writes:
            w.w = (e, val)
            w.r = {}
        return ins

    def dma(self, out_ap, in_ap, reads=(), writes=(), e="sp"):
        if self.dead:
            return None
        k = self.dkeys[self.dma_i % len(self.dkeys)]
        self.dma_i += 1
        deps = self._deps("__dma__", reads, writes)
        if self.cnt[k] > 0:
            self._add(deps, (k, self.cnt[k]))
        self._wait(e, deps)
        ins = self.eng[e].dma_start(out=out_ap, in_=in_ap)
        self.ninstr += 1
        self.cnt[k] += 16
        ins.then_inc(self.sem[k], 16)
        val = self.cnt[k]
        for r in reads:
            r.r[k] = val
        for w in writes:
            w.w = (k, val)
            w.r = {}
        return ins

    def barrier(self):
        if self.dead:
            return
        for e in self.ENG:
            deps = {}
            for k in list(self.ENG) + self.dkeys:
                if k != e and self.cnt[k] > 0:
                    deps[k] = self.cnt[k]
            self._wait(e, deps)


SC_CQ, SC_CKV, SC_KPE, SC_MLAZ, SC_CONV = 0, 512, 1024, 1088, 1856
SC_DQ, SC_DK, SC_DV, SC_IW, SC_DSAZ, SC_IQ, SC_IK = 3904, 4672, 4928, 5184, 5200, 5968, 6992
PIECES = [
    ("cq", 0, 512), ("ckv", 512, 512), ("kpez", 1024, 448), ("mlaz2", 1472, 384),
    ("conv0", 1856, 512), ("conv1", 2368, 512), ("conv2", 2880, 512), ("conv3", 3392, 512),
    ("dq0", 3904, 512), ("dq1dk", 4416, 512), ("dviw", 4928, 272), ("dsaz0", 5200, 512),
    ("dsaz1iq", 5712, 512), ("iq1", 6224, 512), ("iq2ik", 6736, 384),
]
SP_GQ, SP_GKV, SP_QN_N, SP_QN_R, SP_KN_N, SP_KN_R, SP_DQN, SP_DKN, SP_CONV, NSP = 0, 4, 8, 9, 10, 11, 12, 13, 14, 26
CB_ID, CB_ONES, CB_PMLA, CB_PDSA, CB_PIDX, CB_TRI, NCB = 0, 128, 256, 384, 512, 640, 768
CF_IBIAS, CF_INV, CF_EPS, CF_ONE, CF_ONESF, NCF = 0, 128, 131, 132, 133, 133 + 128


def build(S=4096, depth=2, debug=False, stop_after=None):
    nc = bass.Bass("TRN2", target_bir_lowering=False)
    NG = S // 512
    NT = S // 128
    kI = "ExternalOutput" if debug else "Internal"

    def dram(name, shape, dt, kind):
        return nc.dram_tensor(name, shape, dt, kind=kind).ap()

    x_d = dram("x", [S, 2048], F32, "ExternalInput")
    p_d = dram("p", [depth, S, 256], F32, "ExternalInput")
    pos_d = dram("pos", [1, S], I32, "ExternalInput")
    w_in_d = dram("w_in", [depth, 2048, N_IN], F32, "ExternalInput")
    w_uq_d = dram("w_uq", [depth, 512, 1152], F32, "ExternalInput")
    w_ukv_d = dram("w_ukv", [depth, 512, 1536], F32, "ExternalInput")
    w_out_d = dram("w_out", [depth, 2048, 2048], F32, "ExternalInput")
    w_pg_d = dram("w_pg", [depth, 2048, 2048], F32, "ExternalInput")
    w_pp_d = dram("w_pp", [depth, 256, 2048], F32, "ExternalInput")
    sp_d = dram("sp", [depth, 128, NSP], F32, "ExternalInput")
    rowp_d = dram("rowp", [depth * 2, 2048], F32, "ExternalInput")
    cb_d = dram("cb", [128, NCB], BF16, "ExternalInput")
    cf_d = dram("cf", [128, NCF], F32, "ExternalInput")
    y_d = dram("y", [S, 2048], F32, "ExternalOutput")

    tab_d = dram("tab", [6, 128, S], F32, kI)
    wsc_in_l = [dram("wsc_in%d" % i, [128, 16, N_SC], BF16, "Internal") for i in range(depth)]
    wsc_uq_l = [dram("wsc_uq%d" % i, [128, 4, 1152], BF16, "Internal") for i in range(depth)]
    wsc_ukv_l = [dram("wsc_ukv%d" % i, [128, 4, 1536], BF16, "Internal") for i in range(depth)]
    wsc_out_l = [dram("wsc_out%d" % i, [128, 16, 2048], BF16, "Internal") for i in range(depth)]
    wsc_pg_l = [dram("wsc_pg%d" % i, [128, 16, 2048], BF16, "Internal") for i in range(depth)]
    wsc_pp_l = [dram("wsc_pp%d" % i, [128, 2, 2048], BF16, "Internal") for i in range(depth)]
    h1_d = dram("h1", [S, 2048], F32, kI)
    qn_d = dram("qn", [6, 128, S], BF16, kI)
    qr_d = dram("qr", [6, 128, S], BF16, kI)
    kn_d = dram("kn", [6, 128, S], BF16, kI)
    kr_d = dram("kr", [6, 128, S], BF16, kI)
    vm_d = dram("vm", [6, 128, NT, 128], BF16, kI)
    gm_d = dram("gm", [768, S], F32, kI)
    dq_d = dram("dq", [6, 128, S], BF16, kI)
    dk_d = dram("dk", [2, 128, S], BF16, kI)
    dv_d = dram("dv", [2, 128, NT, 128], BF16, kI)
    gd_d = dram("gd", [768, S], F32, kI)
    iq_d = dram("iq", [8, 128, S], BF16, kI)
    ik_d = dram("ik", [128, S], BF16, kI)
    iw_d = dram("iw", [S, 32], F32, kI)
    mix_d = dram("mix", [2048, S], BF16, kI)

    es_top = ExitStack()
    with es_top:
        fw = FW(nc, es_top)
        op, dma = fw.op, fw.dma

        sb_uid = [0]

        def ck(name):
            if stop_after == name:
                fw.dead = True

        def sb(es, name, shape, dt, nres=1):
            sb_uid[0] += 1
            nm = "sb%d_%s" % (sb_uid[0], name)
            return Tile(es.enter_context(nc.sbuf_tensor(nm, shape, dt)), nm, nres)

        banks = [Tile(es_top.enter_context(nc.psum_tensor("bank%d" % i, [128, 512], F32)), "bank%d" % i)
                 for i in range(8)]
        bank_i = [0]

        def bank():
            b = banks[bank_i[0] % 8]
            bank_i[0] += 1
            return b

        cb = sb(es_top, "cb", [128, NCB], BF16)
        cf = sb(es_top, "cf", [128, NCF], F32)
        dma(cb[:], cb_d[:, :], writes=[cb.r])
        dma(cf[:], cf_d[:, :], writes=[cf.r])
        ident = cb[:, CB_ID:CB_ID + 128]
        ones = cb[:, CB_ONES:CB_ONES + 128]
        epsc = cf[:, CF_EPS:CF_EPS + 1]
        onec = cf[:, CF_ONE:CF_ONE + 1]

        r_tab = R("tab")
        r_w_l = [{n: R("wsc_%s%d" % (n, i)) for n in ("in", "uq", "ukv", "out", "pg", "pp")} for i in range(depth)]
        r_h = [R("h%d" % i) for i in range(depth + 1)]
        r_A = R("Aout")
        r_mix = R("mix")

        def cast_units(l):
            wsc_in, wsc_uq, wsc_ukv = wsc_in_l[l], wsc_uq_l[l], wsc_ukv_l[l]
            wsc_out, wsc_pg, wsc_pp = wsc_out_l[l], wsc_pg_l[l], wsc_pp_l[l]
            rw = r_w_l[l]
            units = []

            def mk(src_ap, n, stores, r):
                def ld(stg):
                    dma(stg[:, 0:n], src_ap, writes=[stg.r])

                def u(stg, stb, ce):
                    if ce == "act":
                        op("act", lambda e: e.activation(out=stb[:, 0:n], in_=stg[:, 0:n], func=AF.Copy),
                           reads=[stg.r], writes=[stb.r])
                    else:
                        op(ce, lambda e: e.tensor_copy(out=stb[:, 0:n], in_=stg[:, 0:n]), reads=[stg.r], writes=[stb.r])
                    for f in stores:
                        d_ap, s_ap = f(stb)
                        dma(d_ap, s_ap, reads=[stb.r], writes=[r])
                units.append((ld, u))

            for kc in range(16):
                src = w_in_d[l, kc * 128:(kc + 1) * 128, :]
                mk(src[:, 0:1856], 1856, [lambda t, kc=kc: (wsc_in[:, kc, 0:1856], t[:, 0:1856])], rw["in"])
                mk(src[:, 1856:3904], 2048,
                   [(lambda t, kind=kind, kc=kc: (
                       wsc_in[:, kc, 1856:3904].rearrange("p (j k c) -> p j k c", k=4, c=128)[:, :, kind, :],
                       t[:, kind * 512:(kind + 1) * 512].rearrange("p (j c) -> p j c", c=128)))
                    for kind in range(4)], rw["in"])
                mk(src[:, 3904:5184], 1280, [lambda t, kc=kc: (wsc_in[:, kc, 3904:5184], t[:, 0:1280])], rw["in"])
                mk(src[:, 5184:7056], 1872,
                   [lambda t, kc=kc: (wsc_in[:, kc, SC_DSAZ:SC_DSAZ + 1792], t[:, 0:1792]),
                    lambda t, kc=kc: (wsc_in[:, kc, SC_IW:SC_IW + 16], t[:, 1792:1808]),
                    lambda t, kc=kc: (wsc_in[:, kc, SC_IK:SC_IK + 64], t[:, 1808:1872]),
                    lambda t, kc=kc: (wsc_in[:, kc, SC_IK + 64:SC_IK + 128], t[:, 1808:1872])], rw["in"])
            for kc in range(4):
                mk(w_uq_d[l, kc * 128:(kc + 1) * 128, :], 1152, [lambda t, kc=kc: (wsc_uq[:, kc, :], t[:, 0:1152])], rw["uq"])
                mk(w_ukv_d[l, kc * 128:(kc + 1) * 128, :], 1536,
                   [(lambda t, two=two, kc=kc: (
                       wsc_ukv[:, kc, two * 768:(two + 1) * 768].rearrange("p (h c) -> p h c", c=128),
                       t[:, 0:1536].rearrange("p (h two c) -> p h two c", two=2, c=128)[:, :, two, :]))
                    for two in range(2)], rw["ukv"])
            for kc in range(16):
                mk(w_out_d[l, kc * 128:(kc + 1) * 128, :], 2048, [lambda t, kc=kc: (wsc_out[:, kc, :], t[:, :])], rw["out"])
                mk(w_pg_d[l, kc * 128:(kc + 1) * 128, :], 2048, [lambda t, kc=kc: (wsc_pg[:, kc, :], t[:, :])], rw["pg"])
            for kc in range(2):
                mk(w_pp_d[l, kc * 128:(kc + 1) * 128, :], 2048, [lambda t, kc=kc: (wsc_pp[:, kc, :], t[:, :])], rw["pp"])
            return units

        with ExitStack() as es:
            NB_ = 6
            stg = [sb(es, "wstg%d" % i, [128, 2048], F32) for i in range(NB_)]
            stb = [sb(es, "wstb%d" % i, [128, 2048], BF16) for i in range(NB_)]
            posi = sb(es, "posi", [128, S], I32)
            posf = sb(es, "posf", [128, S], F32)
            ang = sb(es, "ang", [128, S], F32)
            ti = sb(es, "ti", [128, S], I32)
            tf = sb(es, "tf", [128, S], F32)
            rr = sb(es, "rr", [128, S], F32)
            mm = sb(es, "mm", [128, S], F32)
            tout = sb(es, "tout", [128, S], F32)

            def setup_gen():
                dma(posi[:], pos_d.partition_broadcast(128).rearrange("p o s -> p (o s)"), writes=[posi.r])
                op("dve", lambda e: e.tensor_copy(out=posf[:], in_=posi[:]), reads=[posi.r], writes=[posf.r])
                yield
                C1 = 6.28125
                C2 = 2.0 * PI - C1
                for typ in range(3):
                    for cs in range(2):
                        shift = 0.0 if cs == 0 else PI / 2.0
                        op("dve", lambda e: e.tensor_scalar(out=ang[:], in0=posf[:], scalar1=cf[:, CF_INV + typ:CF_INV + typ + 1],
                                                            scalar2=shift, op0=ALU.mult, op1=ALU.add),
                           reads=[posf.r, cf.r], writes=[ang.r])
                        yield
                        op("dve", lambda e: e.tensor_scalar(out=ti[:], in0=ang[:], scalar1=1.0 / (2.0 * PI), scalar2=None,
                                                            op0=ALU.mult), reads=[ang.r], writes=[ti.r])
                        yield
                        op("dve", lambda e: e.tensor_copy(out=tf[:], in_=ti[:]), reads=[ti.r], writes=[tf.r])
                        yield
                        op("dve", lambda e: e.scalar_tensor_tensor(out=rr[:], in0=tf[:], scalar=-C1, in1=ang[:],
                                                                   op0=ALU.mult, op1=ALU.add),
                           reads=[tf.r, ang.r], writes=[rr.r])
                        yield
                        op("dve", lambda e: e.scalar_tensor_tensor(out=ang[:], in0=tf[:], scalar=-C2, in1=rr[:],
                                                                   op0=ALU.mult, op1=ALU.add),
                           reads=[tf.r, rr.r], writes=[ang.r])
                        yield
                        op("dve", lambda e: e.tensor_scalar(out=mm[:], in0=ang[:], scalar1=PI, scalar2=None, op0=ALU.is_gt),
                           reads=[ang.r], writes=[mm.r])
                        yield
                        op("dve", lambda e: e.scalar_tensor_tensor(out=rr[:], in0=mm[:], scalar=-2.0 * PI, in1=ang[:],
                                                                   op0=ALU.mult, op1=ALU.add),
                           reads=[mm.r, ang.r], writes=[rr.r])
                        yield
                        op("dve", lambda e: e.tensor_scalar(out=mm[:], in0=rr[:], scalar1=-PI, scalar2=None, op0=ALU.is_lt),
                           reads=[rr.r], writes=[mm.r])
                        yield
                        op("dve", lambda e: e.scalar_tensor_tensor(out=ang[:], in0=mm[:], scalar=2.0 * PI, in1=rr[:],
                                                                   op0=ALU.mult, op1=ALU.add),
                           reads=[mm.r, rr.r], writes=[ang.r])
                        yield
                        op("dve", lambda e: e.tensor_scalar(out=rr[:], in0=ang[:], scalar1=3.141592, scalar2=-3.141592,
                                                            op0=ALU.min, op1=ALU.max), reads=[ang.r], writes=[rr.r])
                        yield
                        op("act", lambda e: e.activation(out=tout[:], in_=rr[:], func=AF.Sin), reads=[rr.r], writes=[tout.r])
                        dma(tab_d[typ * 2 + cs], tout[:], reads=[tout.r], writes=[r_tab])
                        yield

            sg = setup_gen()
            cu0 = cast_units(0)
            PF = 4
            for i in range(min(PF, len(cu0))):
                cu0[i][0](stg[i % NB_])
            for i in range(len(cu0)):
                if i + PF < len(cu0):
                    cu0[i + PF][0](stg[(i + PF) % NB_])
                cu0[i][1](stg[i % NB_], stb[i % NB_], ("pool", "act", "act")[i % 3])
                next(sg, None)
            for _ in sg:
                pass
            fw.barrier()
        if stop_after == "setup":
            fw.barrier()
            return nc

        def phase_a(l, h_src, r_hsrc):
            wsc_in, wsc_uq, wsc_ukv = wsc_in_l[l], wsc_uq_l[l], wsc_ukv_l[l]
            r_w = r_w_l[l]
            with ExitStack() as es:
                spt = sb(es, "spt", [128, NSP], F32)
                gb = sb(es, "gb", [128, 2048], F32)
                wuq = sb(es, "wuq", [128, 4, 1152], BF16)
                wukv = sb(es, "wukv", [128, 4, 1536], BF16)
                aT = sb(es, "aT", [128, 16, 512], BF16)
                htile = [sb(es, "ht%d" % i, [128, 2048], F32) for i in range(2)]
                an = [sb(es, "an%d" % i, [128, 2048], BF16) for i in range(4)]
                st = [sb(es, "st%d" % i, [128, 4], F32) for i in range(2)]
                wbuf = [sb(es, "wb%d" % i, [128, 16, 512], BF16) for i in range(3)]
                tabs = [sb(es, "tab%d" % i, [128, 512], F32) for i in range(6)]
                cqn = sb(es, "cqn", [128, 4, 512], BF16)
                ckvn = sb(es, "ckvn", [128, 4, 512], BF16)
                sq = [sb(es, "sq%d" % i, [128, 512], BF16) for i in range(6)]
                sq64 = [sb(es, "sq64_%d" % i, [128, 512], BF16) for i in range(2)]
                sqkpe_t = sb(es, "sqkpe", [128, 512], BF16)
                f32t = [sb(es, "f32t%d" % i, [128, 512], F32) for i in range(12)]
                b16t = [sb(es, "b16t%d" % i, [128, 512], BF16) for i in range(8)]
                pad64 = [sb(es, "pad64_%d" % i, [128, 512], BF16) for i in range(3)]
                krot = sb(es, "krot", [128, 512], F32)
                uconv = [sb(es, "uconv%d" % j, [128, 516], F32) for j in range(4)]
                tmo = [sb(es, "tmo%d" % i, [128, 768], BF16) for i in range(2)]
                iwt = [sb(es, "iwt%d" % i, [128, 32], F32) for i in range(2)]
                rot = {"sq": 0, "sq64": 0, "f": 0, "b": 0, "pad": 0, "w": 0, "tmo": 0, "iwt": 0}

                def nxt(lst, key):
                    t = lst[rot[key] % len(lst)]
                    rot[key] += 1
                    return t

                dma(spt[:], sp_d[l], writes=[spt.r])
                dma(gb[:], rowp_d[2 * l:2 * l + 1, :].partition_broadcast(128).rearrange("p o s -> p (o s)"), writes=[gb.r])
                dma(wuq[:], wsc_uq[:, :, :], reads=[r_w["uq"]], writes=[wuq.r])
                dma(wukv[:], wsc_ukv[:, :, :], reads=[r_w["ukv"]], writes=[wukv.r])
                for t_ in sq64 + pad64 + [sqkpe_t]:
                    op("pool", lambda e: e.memset(t_[:], 0.0), writes=[t_.r])
                op("pool", lambda e: e.memset(krot[:], 0.0), writes=[krot.r])
                for j in range(4):
                    op("pool", lambda e: e.memset(uconv[j][:], 0.0), writes=[uconv[j].r])

                ck("A0")

                def col(c):
                    return spt[:, c:c + 1]

                def load_piece(pi):
                    name, c0, n = PIECES[pi]
                    wt = nxt(wbuf, "w")
                    dma(wt[:, :, 0:n], wsc_in[:, :, c0:c0 + n], reads=[r_w["in"]], writes=[wt.r])
                    return wt

                def fm_mm(wt, off, M, src=None, wsrc=None, nk=16):
                    b = bank()
                    a = aT if src is None else src
                    for kc in range(nk):
                        op("pe", lambda e: e.matmul(b[0:M, :], lhsT=wt[:, kc, off:off + M], rhs=a[:, kc, :],
                                                    start=(kc == 0), stop=(kc == nk - 1)),
                           reads=[wt.r, a.r], writes=[b.r], inc=(kc == nk - 1))
                    return b

                def square(b, M):
                    if M == 128:
                        s = nxt(sq, "sq")
                        op("act", lambda e: e.activation(out=s[:], in_=b[:], func=AF.Square), reads=[b.r], writes=[s.r])
                    else:
                        s = nxt(sq64, "sq64")
                        op("act", lambda e: e.activation(out=s[0:M, :], in_=b[0:M, :], func=AF.Square),
                           reads=[b.r], writes=[s.r])
                    return s

                def rstd_from(sqs, nfeat):
                    sb_ = bank()
                    for i, s in enumerate(sqs):
                        op("pe", lambda e: e.matmul(sb_[:], lhsT=ones, rhs=s[:], start=(i == 0), stop=(i == len(sqs) - 1)),
                           reads=[s.r, cb.r], writes=[sb_.r], inc=(i == len(sqs) - 1))
                    t = nxt(f32t, "f")
                    op("act", lambda e: e.activation(out=t[:], in_=sb_[:], func=AF.Ln, scale=1.0 / nfeat, bias=epsc),
                       reads=[sb_.r, cf.r], writes=[t.r])
                    r_ = nxt(f32t, "f")
                    op("act", lambda e: e.activation(out=r_[:], in_=t[:], func=AF.Exp, scale=-0.5),
                       reads=[t.r], writes=[r_.r])
                    return r_

                def scale_out(b, M, gcol, rstd, out_ap, out_r):
                    op("dve", lambda e: e.scalar_tensor_tensor(out=out_ap, in0=b[0:M, :], scalar=col(gcol)[0:M, :],
                                                               in1=rstd[0:M, :], op0=ALU.mult, op1=ALU.mult),
                       reads=[b.r, spt.r, rstd.r], writes=[out_r])

                def rope_a(xf):
                    xb = nxt(b16t, "b")
                    op("act", lambda e: e.activation(out=xb[:], in_=xf[:], func=AF.Copy), reads=[xf.r], writes=[xb.r])
                    return xb

                def rope_b(xf, xb, typ, pmat_col, out_t, M=128):
                    pb = bank()
                    op("pe", lambda e: e.matmul(pb[:], lhsT=cb[:, pmat_col:pmat_col + 128], rhs=xb[:], start=True, stop=True),
                       reads=[cb.r, xb.r], writes=[pb.r])
                    t1 = nxt(f32t, "f")
                    op("pool", lambda e: e.tensor_tensor(out=t1[0:M, :], in0=xf[0:M, :], in1=tabs[typ * 2 + 1][0:M, :], op=ALU.mult),
                       reads=[xf.r, tabs[typ * 2 + 1].r], writes=[t1.r])
                    t2 = nxt(f32t, "f")
                    op("dve", lambda e: e.tensor_tensor(out=t2[0:M, :], in0=pb[0:M, :], in1=tabs[typ * 2][0:M, :], op=ALU.mult),
                       reads=[pb.r, tabs[typ * 2].r], writes=[t2.r])
                    op("pool", lambda e: e.tensor_tensor(out=out_t[0:M, :], in0=t1[0:M, :], in1=t2[0:M, :], op=ALU.add),
                       reads=[t1.r, t2.r], writes=[out_t.r])

                def rope(xf, typ, pmat_col, out_t, M=128):
                    rope_b(xf, rope_a(xf), typ, pmat_col, out_t, M)

                def pipe3(jobs):
                    n = len(jobs)
                    stt_ = [None] * n
                    for t in range(n + 2):
                        if t < n:
                            stt_[t] = jobs[t][0]()
                        if 0 <= t - 1 < n and jobs[t - 1][1] is not None:
                            jobs[t - 1][1](stt_[t - 1])
                        if 0 <= t - 2 < n and jobs[t - 2][2] is not None:
                            jobs[t - 2][2](stt_[t - 2])

                def silu_from(b, out_t):
                    e1 = nxt(f32t, "f")
                    op("act", lambda e: e.activation(out=e1[:], in_=b[:], func=AF.Exp, scale=-1.0), reads=[b.r], writes=[e1.r])
                    e2 = nxt(f32t, "f")
                    op("act", lambda e: e.activation(out=e2[:], in_=e1[:], func=AF.Ln, bias=onec), reads=[e1.r, cf.r], writes=[e2.r])
                    e3 = nxt(f32t, "f")
                    op("act", lambda e: e.activation(out=e3[:], in_=e2[:], func=AF.Exp, scale=-1.0), reads=[e2.r], writes=[e3.r])
                    op("dve", lambda e: e.tensor_tensor(out=out_t[:], in0=b[:], in1=e3[:], op=ALU.mult),
                       reads=[b.r, e3.r], writes=[out_t.r])

                def a1_norm(G):
                    for tt in range(4):
                        row0 = G * 512 + tt * 128
                        ht = htile[tt % 2]
                        dma(ht[:], h_src[row0:row0 + 128, :], reads=[r_hsrc], writes=[ht.r])
                        s_ = st[tt % 2]
                        a_ = an[tt]
                        op("act", lambda e: e.activation(out=a_[:], in_=ht[:], func=AF.Square, accum_out=s_[:, 0:1]),
                           reads=[ht.r], writes=[a_.r, s_.r])
                        op("act", lambda e: e.activation(out=s_[:, 1:2], in_=s_[:, 0:1], func=AF.Ln, scale=1.0 / 2048, bias=epsc),
                           reads=[s_.r, cf.r], writes=[s_.r])
                        op("act", lambda e: e.activation(out=s_[:, 2:3], in_=s_[:, 1:2], func=AF.Exp, scale=-0.5),
                           reads=[s_.r], writes=[s_.r])
                        op("dve", lambda e: e.scalar_tensor_tensor(out=a_[:], in0=ht[:], scalar=s_[:, 2:3], in1=gb[:],
                                                                   op0=ALU.mult, op1=ALU.mult),
                           reads=[ht.r, s_.r, gb.r], writes=[a_.r])

                def a1_transpose():
                    for tt in range(4):
                        a_ = an[tt]
                        for half in range(2):
                            pb = bank()
                            pbb = pb[:].bitcast(BF16)
                            for c in range(8):
                                cc = half * 8 + c
                                op("pe", lambda e: e.transpose(out=pbb[:, c * 128:(c + 1) * 128], in_=a_[:, cc * 128:(cc + 1) * 128],
                                                               identity=ident),
                                   reads=[a_.r, cb.r], writes=[pb.r], inc=(c == 7))
                            dst = aT[:, half * 8:(half + 1) * 8, tt * 128:(tt + 1) * 128]
                            srcv = pbb.rearrange("p (c t) -> p c t", t=128)
                            if half == 0:
                                op("act", lambda e: e.activation(out=dst, in_=srcv, func=AF.Copy), reads=[pb.r], writes=[aT.r])
                            else:
                                op("dve", lambda e: e.tensor_copy(out=dst, in_=srcv), reads=[pb.r], writes=[aT.r])

                a1_norm(0)
                for G in range(NG):
                    gs = slice(G * 512, (G + 1) * 512)
                    for i in range(6):
                        dma(tabs[i][:], tab_d[i, :, gs], reads=[r_tab], writes=[tabs[i].r])
                    a1_transpose()

                    ck("A1")

                    def store(dst_ap, t, M=128, r=r_A):
                        dma(dst_ap, t[0:M, :], reads=[t.r], writes=[r])

                    wts = {0: load_piece(0), 1: load_piece(1)}

                    def piece(pi):
                        if pi + 2 < len(PIECES):
                            wts[pi + 2] = load_piece(pi + 2)
                        return wts.pop(pi)

                    wt = piece(0)
                    bq = [fm_mm(wt, c * 128, 128) for c in range(4)]
                    sqs = [square(b, 128) for b in bq]
                    wt1 = piece(1)
                    rq = rstd_from(sqs, 512)
                    for c in range(4):
                        scale_out(bq[c], 128, SP_GQ + c, rq, cqn[:, c, :], cqn.r)
                    bkv = [fm_mm(wt1, c * 128, 128) for c in range(4)]
                    sqs = [square(b, 128) for b in bkv]
                    rkv = rstd_from(sqs, 512)
                    for c in range(4):
                        scale_out(bkv[c], 128, SP_GKV + c, rkv, ckvn[:, c, :], ckvn.r)
                    ck("Ackv")
                    wt = piece(2)
                    bkpe = fm_mm(wt, 0, 64)
                    sqkpe = sqkpe_t
                    op("act", lambda e: e.activation(out=sqkpe[0:64, :], in_=bkpe[0:64, :], func=AF.Square),
                       reads=[bkpe.r], writes=[sqkpe.r])
                    kx = nxt(f32t, "f")
                    op("pool", lambda e: e.memset(kx[64:128, :], 0.0), writes=[kx.r])
                    op("dve", lambda e: e.tensor_scalar(out=kx[0:64, :], in0=bkpe[0:64, :], scalar1=col(SP_KN_R)[0:64, :],
                                                        scalar2=None, op0=ALU.mult), reads=[bkpe.r, spt.r], writes=[kx.r])
                    rope(kx, 0, CB_PMLA, krot, M=64)

                    def gate_chunks(wt, off0, n, dst, row0):
                        for c in range(n):
                            b = fm_mm(wt, off0 + c * 128, 128)
                            o = nxt(f32t, "f")
                            silu_from(b, o)
                            store(dst[row0 + c * 128:row0 + (c + 1) * 128, gs], o)

                    gate_chunks(wt, 64, 3, gm_d, 0)
                    ck("Akpe")
                    def q_job(h):
                        def s1():
                            return {"bn": fm_mm(wuq, h * 192, 128, src=cqn, nk=4),
                                    "br": fm_mm(wuq, h * 192 + 128, 64, src=cqn, nk=4)}

                        def s2(d):
                            sa = square(d["bn"], 128)
                            sb_ = square(d["br"], 64)
                            r_ = rstd_from([sa, sb_], 192)
                            o = nxt(b16t, "b")
                            scale_out(d["bn"], 128, SP_QN_N, r_, o[:], o.r)
                            store(qn_d[h, :, gs], o)
                            xf = nxt(f32t, "f")
                            op("pool", lambda e: e.memset(xf[64:128, :], 0.0), writes=[xf.r])
                            scale_out(d["br"], 64, SP_QN_R, r_, xf[0:64, :], xf.r)
                            d["xf"] = xf
                            d["xb"] = rope_a(xf)

                        def s3(d):
                            o2 = nxt(pad64, "pad")
                            rope_b(d["xf"], d["xb"], 0, CB_PMLA, o2, M=64)
                            store(qr_d[h, :, gs], o2)
                        return (s1, s2, s3)

                    pipe3([q_job(h) for h in range(6)])
                    wt = piece(3)
                    gate_chunks(wt, 0, 3, gm_d, 384)
                    ck("Aq")
                    def k_job(h):
                        def s1():
                            return {"bn": fm_mm(wukv, h * 128, 128, src=ckvn, nk=4)}

                        def s2(d):
                            sa = square(d["bn"], 128)
                            r_ = rstd_from([sa, sqkpe], 192)
                            o = nxt(b16t, "b")
                            scale_out(d["bn"], 128, SP_KN_N, r_, o[:], o.r)
                            store(kn_d[h, :, gs], o)
                            o2 = nxt(pad64, "pad")
                            op("pool", lambda e: e.tensor_tensor(out=o2[0:64, :], in0=krot[0:64, :], in1=r_[0:64, :], op=ALU.mult),
                               reads=[krot.r, r_.r], writes=[o2.r])
                            store(kr_d[h, :, gs], o2)
                        return (s1, s2, None)

                    pipe3([k_job(h) for h in range(6)])
                    for tt in range(4):
                        b1 = bank()
                        b2 = bank()
                        for kc in range(4):
                            op("pe", lambda e: e.matmul(b1[:], lhsT=ckvn[:, kc, tt * 128:(tt + 1) * 128], rhs=wukv[:, kc, 768:1280],
                                                        start=(kc == 0), stop=(kc == 3)),
                               reads=[ckvn.r, wukv.r], writes=[b1.r], inc=(kc == 3))
                        for kc in range(4):
                            op("pe", lambda e: e.matmul(b2[:, 0:256], lhsT=ckvn[:, kc, tt * 128:(tt + 1) * 128], rhs=wukv[:, kc, 1280:1536],
                                                        start=(kc == 0), stop=(kc == 3)),
                               reads=[ckvn.r, wukv.r], writes=[b2.r], inc=(kc == 3))
                        o = nxt(tmo, "tmo")
                        op("act", lambda e: e.activation(out=o[:, 0:512], in_=b1[:], func=AF.Copy), reads=[b1.r], writes=[o.r])
                        op("dve", lambda e: e.tensor_copy(out=o[:, 512:768], in_=b2[:, 0:256]), reads=[b2.r], writes=[o.r])
                        dma(vm_d[:, :, G * 4 + tt, :].rearrange("h p d -> p h d"),
                            o[:].rearrange("p (h d) -> p h d", d=128), reads=[o.r], writes=[r_A])
                    ck("Ak")
                    if G + 1 < NG:
                        a1_norm(G + 1)
                    for j in range(4):
                        wt = piece(4 + j)
                        bx = fm_mm(wt, 0, 128)
                        bb_ = fm_mm(wt, 128, 128)
                        bc = fm_mm(wt, 256, 128)
                        bz = fm_mm(wt, 384, 128)
                        U = uconv[j]
                        cxs = nxt(f32t, "f")
                        op("act", lambda e: e.activation(out=cxs[:], in_=bx[:], func=AF.Copy), reads=[bx.r], writes=[cxs.r])
                        op("dve", lambda e: e.tensor_tensor(out=U[:, 4:516], in0=bc[:], in1=cxs[:], op=ALU.mult),
                           reads=[bc.r, cxs.r], writes=[U.r])
                        cv = nxt(f32t, "f")
                        wc = SP_CONV + j * 3
                        op("dve", lambda e: e.tensor_scalar(out=cv[:], in0=U[:, 4:516], scalar1=col(wc + 2), scalar2=None, op0=ALU.mult),
                           reads=[U.r, spt.r], writes=[cv.r])
                        cv2 = nxt(f32t, "f")
                        op("dve", lambda e: e.scalar_tensor_tensor(out=cv2[:], in0=U[:, 3:515], scalar=col(wc + 1), in1=cv[:],
                                                                   op0=ALU.mult, op1=ALU.add), reads=[U.r, spt.r, cv.r], writes=[cv2.r])
                        cv3 = nxt(f32t, "f")
                        op("dve", lambda e: e.scalar_tensor_tensor(out=cv3[:], in0=U[:, 2:514], scalar=col(wc), in1=cv2[:],
                                                                   op0=ALU.mult, op1=ALU.add), reads=[U.r, spt.r, cv2.r], writes=[cv3.r])
                        hal = nxt(f32t, "f")
                        op("pool", lambda e: e.tensor_copy(out=hal[:, 0:2], in_=U[:, 514:516]), reads=[U.r], writes=[hal.r])
                        op("pool", lambda e: e.tensor_copy(out=U[:, 2:4], in_=hal[:, 0:2]), reads=[hal.r], writes=[U.r])
                        sz = nxt(f32t, "f")
                        silu_from(bz, sz)
                        t3 = nxt(f32t, "f")
                        op("dve", lambda e: e.tensor_tensor(out=t3[:], in0=bb_[:], in1=cv3[:], op=ALU.mult),
                           reads=[bb_.r, cv3.r], writes=[t3.r])
                        o = nxt(b16t, "b")
                        op("pool", lambda e: e.tensor_tensor(out=o[:], in0=t3[:], in1=sz[:], op=ALU.mult),
                           reads=[t3.r, sz.r], writes=[o.r])
                        store(mix_d[768 + j * 128:768 + (j + 1) * 128, gs], o, r=r_mix)
                    ck("Aconv")
                    wcache = {}

                    def getw(pi):
                        if pi not in wcache:
                            wcache[pi] = piece(pi)
                        return wcache[pi]

                    def dsa_job(pi, off, gcol, dst):
                        def s1():
                            return {"b": fm_mm(getw(pi), off, 128)}

                        def s2(d):
                            sa = square(d["b"], 128)
                            r_ = rstd_from([sa], 128)
                            xf = nxt(f32t, "f")
                            scale_out(d["b"], 128, gcol, r_, xf[:], xf.r)
                            d["xf"] = xf
                            d["xb"] = rope_a(xf)

                        def s3(d):
                            o = nxt(b16t, "b")
                            rope_b(d["xf"], d["xb"], 1, CB_PDSA, o)
                            store(dst, o)
                        return (s1, s2, s3)

                    jobs = [dsa_job(8, h * 128, SP_DQN, dq_d[h, :, gs]) for h in range(4)]
                    jobs += [dsa_job(9, h * 128, SP_DQN, dq_d[4 + h, :, gs]) for h in range(2)]
                    jobs += [dsa_job(9, 256 + g * 128, SP_DKN, dk_d[g, :, gs]) for g in range(2)]
                    pipe3(jobs)
                    ck("Adq")
                    wt = piece(10)
                    for tt in range(4):
                        b = bank()
                        for kc in range(16):
                            op("pe", lambda e: e.matmul(b[:, 0:272], lhsT=aT[:, kc, tt * 128:(tt + 1) * 128], rhs=wt[:, kc, 0:272],
                                                        start=(kc == 0), stop=(kc == 15)),
                               reads=[aT.r, wt.r], writes=[b.r], inc=(kc == 15))
                        o = nxt(tmo, "tmo")
                        op("act", lambda e: e.activation(out=o[:, 0:256], in_=b[:, 0:256], func=AF.Copy), reads=[b.r], writes=[o.r])
                        row0 = G * 512 + tt * 128
                        dma(dv_d[:, :, G * 4 + tt, :].rearrange("h p d -> p h d"),
                            o[:, 0:256].rearrange("p (h d) -> p h d", d=128), reads=[o.r], writes=[r_A])
                        w_ = nxt(iwt, "iwt")
                        op("act", lambda e: e.activation(out=w_[:, 0:16], in_=b[:, 256:272], func=AF.Abs, scale=0.25),
                           reads=[b.r], writes=[w_.r])
                        op("act", lambda e: e.activation(out=w_[:, 16:32], in_=b[:, 256:272], func=AF.Sign),
                           reads=[b.r], writes=[w_.r])
                        dma(iw_d[row0:row0 + 128, :], w_[:], reads=[w_.r], writes=[r_A])
                    ck("Adv")
                    wt = piece(11)
                    gate_chunks(wt, 0, 4, gd_d, 0)
                    wt = piece(12)
                    gate_chunks(wt, 0, 2, gd_d, 512)

                    wcache[12] = wt

                    def idx_job(pi, off, scale, dst):
                        def s1():
                            return {"b": fm_mm(getw(pi), off, 128)}

                        def s2(d):
                            xf = nxt(f32t, "f")
                            op("act", lambda e: e.activation(out=xf[:], in_=d["b"][:], func=AF.Identity, scale=scale),
                               reads=[d["b"].r], writes=[xf.r])
                            d["xf"] = xf
                            d["xb"] = rope_a(xf)

                        def s3(d):
                            o = nxt(b16t, "b")
                            rope_b(d["xf"], d["xb"], 2, CB_PIDX, o)
                            store(dst, o)
                        return (s1, s2, s3)

                    jobs = [idx_job(12, 256 + c * 128, 0.125, iq_d[c, :, gs]) for c in range(2)]
                    jobs += [idx_job(13, c * 128, 0.125, iq_d[2 + c, :, gs]) for c in range(4)]
                    jobs += [idx_job(14, c * 128, 0.125, iq_d[6 + c, :, gs]) for c in range(2)]
                    jobs += [idx_job(14, 256, 1.0, ik_d[:, gs])]
                    pipe3(jobs)
                fw.barrier()


        def attention(es, nheads, load_head, scale, masker, gate_d, mix_row0, pfx, den_dve=False):
            pt = [sb(es, pfx + "pt%d" % i, [128, 512], BF16) for i in range(4)]
            gt = [sb(es, pfx + "gt%d" % i, [128, 512], F32) for i in range(2)]
            rd = [sb(es, pfx + "rd%d" % i, [128, 512], F32) for i in range(2)]
            of = [sb(es, pfx + "of%d" % i, [128, 512], F32) for i in range(2)]
            ob = [sb(es, pfx + "ob%d" % i, [128, 512], BF16) for i in range(2)]
            dsum = [sb(es, pfx + "ds%d" % i, [128, 512], F32) for i in range(2)] if den_dve else None
            onesf = cf[:, CF_ONESF:CF_ONESF + 128]
            st_i = [0]
            it = [0]

            def run(h, Gq, hd):
                i2 = it[0] % 2
                it[0] += 1
                oacc = banks[4 + 2 * i2]
                dacc = banks[5 + 2 * i2]
                gs = slice(Gq * 512, (Gq + 1) * 512)
                g_ = gt[i2]
                dma(g_[:], gate_d[h * 128:(h + 1) * 128, gs], reads=[r_A], writes=[g_.r])
                nkc = 4 * (Gq + 1)
                qo = hd["qoff"](Gq)

                def qk(kc):
                    q0 = max(kc - 4 * Gq, 0) * 128
                    stb = banks[st_i[0] % 4]
                    st_i[0] += 1
                    two = hd["kr"] is not None
                    op("pe", lambda e: e.matmul(stb[:, q0:512], lhsT=hd["kn"][:, kc * 128:(kc + 1) * 128],
                                                rhs=hd["qn"][:, qo + q0:qo + 512], start=True, stop=not two),
                       reads=[hd["kn"].r, hd["qn"].r], writes=[stb.r], inc=not two)
                    if two:
                        op("pe", lambda e: e.matmul(stb[:, q0:512], lhsT=hd["kr"][:, kc * 128:(kc + 1) * 128],
                                                    rhs=hd["qr"][:, qo + q0:qo + 512], start=False, stop=True),
                           reads=[hd["kr"].r, hd["qr"].r], writes=[stb.r])
                    p = pt[(st_i[0] - 1) % 4]
                    op("act", lambda e: e.activation(out=p[:, q0:512], in_=stb[:, q0:512], func=AF.Exp, scale=scale),
                       reads=[stb.r], writes=[p.r])
                    masker(p, kc, q0, Gq)
                    return p, q0

                def pv(kc, p, q0):
                    op("pe", lambda e: e.matmul(oacc[:, q0:512], lhsT=hd["v"][:, kc, :], rhs=p[:, q0:512],
                                                start=(kc == 0), stop=(kc == nkc - 1)),
                       reads=[hd["v"].r, p.r], writes=[oacc.r], inc=den_dve)
                    if den_dve:
                        ds = dsum[i2]
                        if kc == 0:
                            op("dve", lambda e: e.tensor_copy(out=ds[:], in_=p[:]), reads=[p.r], writes=[ds.r])
                        else:
                            op("dve", lambda e: e.tensor_tensor(out=ds[:, q0:512], in0=ds[:, q0:512], in1=p[:, q0:512], op=ALU.add),
                               reads=[ds.r, p.r], writes=[ds.r])
                        if kc == nkc - 1:
                            op("pe", lambda e: e.matmul(dacc[:], lhsT=onesf, rhs=ds[:], start=True, stop=True),
                               reads=[cf.r, ds.r], writes=[dacc.r])
                    else:
                        op("pe", lambda e: e.matmul(dacc[:, q0:512], lhsT=ones, rhs=p[:, q0:512],
                                                    start=(kc == 0), stop=(kc == nkc - 1)),
                           reads=[cb.r, p.r], writes=[dacc.r])

                pend = [qk(0)]
                for kc in range(nkc):
                    if kc + 1 < nkc:
                        pend.append(qk(kc + 1))
                    p, q0 = pend.pop(0)
                    pv(kc, p, q0)
                r_ = rd[i2]
                op("act", lambda e: e.activation(out=r_[:], in_=dacc[:], func=AF.Ln), reads=[dacc.r], writes=[r_.r])
                op("act", lambda e: e.activation(out=r_[:], in_=r_[:], func=AF.Exp, scale=-1.0), reads=[r_.r], writes=[r_.r])
                o_ = of[i2]
                op("dve", lambda e: e.tensor_tensor(out=o_[:], in0=oacc[:], in1=r_[:], op=ALU.mult),
                   reads=[oacc.r, r_.r], writes=[o_.r])
                b_ = ob[i2]
                op("pool", lambda e: e.tensor_tensor(out=b_[:], in0=o_[:], in1=g_[:], op=ALU.mult),
                   reads=[o_.r, g_.r], writes=[b_.r])
                dma(mix_d[mix_row0 + h * 128:mix_row0 + (h + 1) * 128, gs], b_[:], reads=[b_.r], writes=[r_mix])

            return run

        def phase_b(l):
            with ExitStack() as es:
                cunits = cast_units(l + 1) if l + 1 < depth else []
                NCB_ = 6
                cstg = [sb(es, "b_wstg%d" % i, [128, 2048], F32) for i in range(NCB_)]
                cstb = [sb(es, "b_wstb%d" % i, [128, 2048], BF16) for i in range(3)]
                cu_i = [0]
                ld_i = [0]

                def emit_loads(upto):
                    while ld_i[0] < min(upto, len(cunits)):
                        i = ld_i[0]
                        cunits[i][0](cstg[i % NCB_])
                        ld_i[0] += 1

                def emit_casts(k):
                    for _ in range(k):
                        if cu_i[0] < len(cunits):
                            i = cu_i[0]
                            emit_loads(i + 1)
                            cunits[i][1](cstg[i % NCB_], cstb[i % 3], "dve")
                            cu_i[0] += 1
                    emit_loads(cu_i[0] + 3)

                hb = []
                for i in range(2):
                    hb.append({
                        "kn": sb(es, "b_kn%d" % i, [128, S], BF16), "kr": sb(es, "b_kr%d" % i, [128, S], BF16),
                        "qn": sb(es, "b_qn%d" % i, [128, S], BF16), "qr": sb(es, "b_qr%d" % i, [128, S], BF16),
                        "v": sb(es, "b_v%d" % i, [128, NT, 128], BF16), "qoff": (lambda Gq: Gq * 512)})
                tri = cb[:, CB_TRI:CB_TRI + 128]

                def masker(p, kc, q0, Gq):
                    if kc >= 4 * Gq:
                        op("pool", lambda e: e.tensor_tensor(out=p[:, q0:q0 + 128], in0=p[:, q0:q0 + 128], in1=tri, op=ALU.mult),
                           reads=[p.r, cb.r], writes=[p.r])

                run = attention(es, 6, None, 192.0 ** -0.5, masker, gm_d, 0, "b_", den_dve=False)

                def load(h):
                    d = hb[h % 2]
                    dma(d["kn"][:], kn_d[h], reads=[r_A], writes=[d["kn"].r])
                    dma(d["kr"][:], kr_d[h], reads=[r_A], writes=[d["kr"].r])
                    dma(d["qn"][:], qn_d[h], reads=[r_A], writes=[d["qn"].r])
                    dma(d["qr"][:], qr_d[h], reads=[r_A], writes=[d["qr"].r])
                    dma(d["v"][:], vm_d[h], reads=[r_A], writes=[d["v"].r])
                    return d

                nxt_h = load(0)
                for h in range(6):
                    hd = nxt_h
                    if h + 1 < 6:
                        nxt_h = load(h + 1)
                    for Gq in range(NG):
                        run(h, Gq, hd)
                        emit_casts(3)
                emit_casts(len(cunits))
                fw.barrier()

        def phase_c(l):
            with ExitStack() as es:
                ik2 = sb(es, "c_ik2", [128, S], BF16)
                dk = [sb(es, "c_dk%d" % g, [128, S], BF16) for g in range(2)]
                dv = [sb(es, "c_dv%d" % g, [128, NT, 128], BF16) for g in range(2)]
                iqt = [sb(es, "c_iq%d" % c, [128, 512], BF16) for c in range(8)]
                dqt = [[sb(es, "c_dq%d_%d" % (i, h), [128, 512], BF16) for h in range(6)] for i in range(2)]
                iwt = [sb(es, "c_iw%d" % i, [128, 32], F32) for i in range(4)]
                wsgs = [sb(es, "c_wsg%d" % i, [128, 16], F32) for i in range(4)]
                sc = sb(es, "c_sc", [128, S], F32)
                scp = sb(es, "c_scp", [128, S], F32)
                rtp = [sb(es, "c_rtp%d" % i, [128, 512], F32) for i in range(2)]
                mqs = [sb(es, "c_mq%d" % i, [128, S], BF16) for i in range(2)]
                maskTs = [sb(es, "c_maskT%d" % i, [128, NT, 512], BF16) for i in range(2)]
                rt = [sb(es, "c_rt%d" % i, [128, 512], F32) for i in range(4)]
                bs = [sb(es, "c_bs%d" % i, [128, 8], F32) for i in range(2)]
                wtab = [sb(es, "c_wtab%d" % i, [128, NBISECT + 1], F32) for i in range(2)]
                p2 = sb(es, "c_p2", [128, NBISECT + 1], F32)
                tri = cb[:, CB_TRI:CB_TRI + 128]
                dma(ik2[:], ik_d[:, :], reads=[r_A], writes=[ik2.r])
                for g in range(2):
                    dma(dk[g][:], dk_d[g], reads=[r_A], writes=[dk[g].r])
                    dma(dv[g][:], dv_d[g], reads=[r_A], writes=[dv[g].r])
                for i in range(NBISECT + 1):
                    op("pool", lambda e: e.memset(p2[:, i:i + 1], 2.0 ** -(i + 1)), writes=[p2.r])
                wb_i = [0]

                def wbank():
                    b = banks[wb_i[0] % 4]
                    wb_i[0] += 1
                    return b

                rt_i = [0]
                rtp_i = [0]
                NPH = 3

                def idx_part1(qb, qbl):
                    if qb < 2:
                        return
                    qs = slice(qbl * 128, (qbl + 1) * 128)
                    nk = (qb + 1) * 128
                    w_ = iwt[qbl]
                    wsg = wsgs[qbl]
                    op("dve", lambda e: e.tensor_tensor(out=wsg[:], in0=w_[:, 0:16], in1=w_[:, 16:32], op=ALU.mult),
                       reads=[w_.r], writes=[wsg.r])
                    mq = mqs[qb % 2]
                    for k0 in range(0, nk, 512):
                        n = min(512, nk - k0)
                        for hh in range(16):
                            c = hh // 2
                            base = (hh % 2) * 64
                            b = wbank()
                            op("pe", lambda e: e.matmul(b[:, 0:n], lhsT=iqt[c][base:base + 64, qs], rhs=ik2[base:base + 64, k0:k0 + n],
                                                        start=True, stop=True), reads=[iqt[c].r, ik2.r], writes=[b.r])
                            onpool = hh < NPH
                            if onpool:
                                r_ = rtp[rtp_i[0] % 2]
                                rtp_i[0] += 1
                            else:
                                r_ = rt[rt_i[0] % 4]
                                rt_i[0] += 1
                            op("act", lambda e: e.activation(out=r_[:, 0:n], in_=b[:, 0:n], func=AF.Relu),
                               reads=[b.r], writes=[r_.r])
                            if onpool:
                                if hh == 0:
                                    op("pool", lambda e: e.tensor_scalar(out=scp[:, k0:k0 + n], in0=r_[:, 0:n], scalar1=wsg[:, 0:1],
                                                                         scalar2=None, op0=ALU.mult), reads=[r_.r, wsg.r], writes=[scp.r])
                                else:
                                    op("pool", lambda e: e.tensor_scalar(out=r_[:, 0:n], in0=r_[:, 0:n], scalar1=wsg[:, hh:hh + 1],
                                                                         scalar2=None, op0=ALU.mult), reads=[r_.r, wsg.r], writes=[r_.r])
                                    op("pool", lambda e: e.tensor_tensor(out=scp[:, k0:k0 + n], in0=scp[:, k0:k0 + n], in1=r_[:, 0:n], op=ALU.add),
                                       reads=[scp.r, r_.r], writes=[scp.r])
                            elif hh == NPH:
                                op("dve", lambda e: e.tensor_scalar(out=sc[:, k0:k0 + n], in0=r_[:, 0:n], scalar1=wsg[:, hh:hh + 1],
                                                                    scalar2=None, op0=ALU.mult), reads=[r_.r, wsg.r], writes=[sc.r])
                            else:
                                op("dve", lambda e: e.scalar_tensor_tensor(out=sc[:, k0:k0 + n], in0=r_[:, 0:n], scalar=wsg[:, hh:hh + 1],
                                                                           in1=sc[:, k0:k0 + n], op0=ALU.mult, op1=ALU.add),
                                   reads=[r_.r, wsg.r, sc.r], writes=[sc.r])
                        op("dve", lambda e: e.tensor_tensor(out=sc[:, k0:k0 + n], in0=sc[:, k0:k0 + n], in1=scp[:, k0:k0 + n], op=ALU.add),
                           reads=[sc.r, scp.r], writes=[sc.r])
                    op("dve", lambda e: e.tensor_tensor(out=sc[:, qb * 128:nk], in0=sc[:, qb * 128:nk], in1=cf[:, CF_IBIAS:CF_IBIAS + 128],
                                                        op=ALU.add), reads=[sc.r, cf.r], writes=[sc.r])
                    b_ = bs[qb % 2]
                    wt_ = wtab[qb % 2]
                    op("dve", lambda e: e.tensor_reduce(out=b_[:, 0:1], in_=sc[:, 0:nk], axis=mybir.AxisListType.X, op=ALU.max),
                       reads=[sc.r], writes=[b_.r])
                    op("dve", lambda e: e.tensor_reduce(out=b_[:, 1:2], in_=sc[:, 0:qb * 128], axis=mybir.AxisListType.X, op=ALU.min),
                       reads=[sc.r], writes=[b_.r])
                    op("dve", lambda e: e.tensor_tensor(out=b_[:, 2:3], in0=b_[:, 0:1], in1=b_[:, 1:2], op=ALU.subtract),
                       reads=[b_.r], writes=[b_.r])
                    op("dve", lambda e: e.scalar_tensor_tensor(out=b_[:, 3:4], in0=b_[:, 2:3], scalar=0.5, in1=b_[:, 1:2],
                                                               op0=ALU.mult, op1=ALU.add), reads=[b_.r], writes=[b_.r])
                    op("dve", lambda e: e.tensor_scalar(out=wt_[:], in0=p2[:], scalar1=b_[:, 2:3], scalar2=None, op0=ALU.mult),
                       reads=[p2.r, b_.r], writes=[wt_.r])
                    for i in range(NBISECT):
                        op("dve", lambda e: e.tensor_scalar(out=mq[:, 0:nk], in0=sc[:, 0:nk], scalar1=b_[:, 3:4], scalar2=None,
                                                            op0=ALU.is_ge, op1=ALU.add, accum_out=b_[:, 4:5]),
                           reads=[sc.r, b_.r], writes=[mq.r, b_.r])
                        op("dve", lambda e: e.tensor_scalar(out=b_[:, 5:6], in0=b_[:, 4:5], scalar1=TOPK - 0.5, scalar2=0.5,
                                                            op0=ALU.is_ge, op1=ALU.subtract), reads=[b_.r], writes=[b_.r])
                        op("dve", lambda e: e.scalar_tensor_tensor(out=b_[:, 3:4], in0=b_[:, 5:6], scalar=wt_[:, i:i + 1], in1=b_[:, 3:4],
                                                                   op0=ALU.mult, op1=ALU.add), reads=[b_.r, wt_.r], writes=[b_.r])
                    op("dve", lambda e: e.tensor_tensor(out=b_[:, 6:7], in0=b_[:, 3:4], in1=wt_[:, NBISECT:NBISECT + 1], op=ALU.subtract),
                       reads=[b_.r, wt_.r], writes=[b_.r])
                    op("dve", lambda e: e.tensor_scalar(out=mq[:, 0:nk], in0=sc[:, 0:nk], scalar1=b_[:, 6:7], scalar2=None, op0=ALU.is_ge),
                       reads=[sc.r, b_.r], writes=[mq.r])

                def idx_part2(qb, qbl, maskT):
                    qs = slice(qbl * 128, (qbl + 1) * 128)
                    if qb < 2:
                        for kc in range(qb + 1):
                            src = tri if kc == qb else ones
                            op("pool", lambda e: e.tensor_copy(out=maskT[:, kc, qs], in_=src), reads=[cb.r], writes=[maskT.r])
                        return
                    mq = mqs[qb % 2]
                    for k0 in range(0, qb + 1, 8):
                        n = min(8, qb + 1 - k0)
                        b = wbank()
                        bb_ = b[:].bitcast(BF16)
                        for j in range(n):
                            op("pe", lambda e: e.transpose(out=bb_[:, j * 128:(j + 1) * 128], in_=mq[:, (k0 + j) * 128:(k0 + j + 1) * 128],
                                                           identity=ident), reads=[mq.r, cb.r], writes=[b.r], inc=(j == n - 1))
                        op("act", lambda e: e.activation(out=maskT[:, k0:k0 + n, qs],
                                                         in_=bb_[:, 0:n * 128].rearrange("p (c t) -> p c t", t=128), func=AF.Copy),
                           reads=[b.r], writes=[maskT.r])

                cur_mask = [None]

                def masker(p, kc, q0, Gq):
                    mT = cur_mask[0]
                    op("pool", lambda e: e.tensor_tensor(out=p[:, q0:512], in0=p[:, q0:512], in1=mT[:, kc, q0:512], op=ALU.mult),
                       reads=[p.r, mT.r], writes=[p.r])

                run = attention(es, 6, None, 128.0 ** -0.5, masker, gd_d, 1280, "c_")
                dsa_steps = [[0], [1, 2], [3], [4, 5]]
                for Gq in range(NG + 1):
                    if Gq < NG:
                        gs = slice(Gq * 512, (Gq + 1) * 512)
                        for c in range(8):
                            dma(iqt[c][:], iq_d[c, :, gs], reads=[r_A], writes=[iqt[c].r])
                        for h in range(6):
                            dma(dqt[Gq % 2][h][:], dq_d[h, :, gs], reads=[r_A], writes=[dqt[Gq % 2][h].r])
                        for i in range(4):
                            r0 = Gq * 512 + i * 128
                            dma(iwt[i][:], iw_d[r0:r0 + 128, :], reads=[r_A], writes=[iwt[i].r])
                    for step in range(4):
                        if Gq < NG:
                            idx_part1(Gq * 4 + step, step)
                        if Gq > 0:
                            cur_mask[0] = maskTs[(Gq - 1) % 2]
                            for h in dsa_steps[step]:
                                g = h // 3
                                hd = {"kn": dk[g], "kr": None, "qn": dqt[(Gq - 1) % 2][h], "qr": None, "v": dv[g],
                                      "qoff": (lambda Gq_: 0)}
                                run(h, Gq - 1, hd)
                        if Gq < NG:
                            idx_part2(Gq * 4 + step, step, maskTs[Gq % 2])
                fw.barrier()

        def phase_d(l, h_src, r_hsrc, h_dst, r_hdst):
            wsc_out, wsc_pg, wsc_pp = wsc_out_l[l], wsc_pg_l[l], wsc_pp_l[l]
            r_w = r_w_l[l]
            with ExitStack() as es:
                gbp = sb(es, "d_gbp", [128, 2048], F32)
                wpp = sb(es, "d_wpp", [128, 2, 2048], BF16)
                mx = sb(es, "d_mx", [128, 16, 512], BF16)
                hT2 = [[sb(es, "d_h%d_%d" % (j, i), [128, 2048], F32) for i in range(4)] for j in range(2)]
                pT2 = [[sb(es, "d_p%d_%d" % (j, i), [128, 256], F32) for i in range(4)] for j in range(2)]
                pbf = [sb(es, "d_pbf%d" % i, [128, 256], BF16) for i in range(2)]
                a1 = [sb(es, "d_a1%d" % i, [128, 2048], BF16) for i in range(2)]
                a1T = sb(es, "d_a1T", [128, 16, 512], BF16)
                ppT = sb(es, "d_ppT", [128, 2, 512], BF16)
                wbuf = [sb(es, "d_wb%d" % i, [128, 16, 512], BF16) for i in range(3)]
                st = [sb(es, "d_st%d" % i, [128, 4], F32) for i in range(2)]
                ft = [sb(es, "d_ft%d" % i, [128, 512], F32) for i in range(6)]
                ft_i = [0]
                w_i = [0]

                def nft():
                    t = ft[ft_i[0] % 6]
                    ft_i[0] += 1
                    return t

                dma(gbp[:], rowp_d[2 * l + 1:2 * l + 2, :].partition_broadcast(128).rearrange("p o s -> p (o s)"), writes=[gbp.r])
                dma(wpp[:], wsc_pp[:, :, :], reads=[r_w["pp"]], writes=[wpp.r])

                def loadw(src, rw, cbk):
                    wt = wbuf[w_i[0] % 3]
                    w_i[0] += 1
                    dma(wt[:], src[:, :, cbk * 512:(cbk + 1) * 512], reads=[rw], writes=[wt.r])
                    return wt

                def d_loads(G):
                    for tt in range(4):
                        r0 = G * 512 + tt * 128
                        dma(hT2[G % 2][tt][:], h_src[r0:r0 + 128, :], reads=[r_hsrc], writes=[hT2[G % 2][tt].r])
                        dma(pT2[G % 2][tt][:], p_d[l, r0:r0 + 128, :], writes=[pT2[G % 2][tt].r])

                def d_load_mx(G):
                    dma(mx[:], mix_d[:, G * 512:(G + 1) * 512].rearrange("(c p) s -> p c s", p=128), reads=[r_mix], writes=[mx.r])

                d_load_mx(0)
                d_loads(0)
                wq = [loadw(wsc_out, r_w["out"], 0), loadw(wsc_out, r_w["out"], 1)]
                for G in range(NG):
                    gs = slice(G * 512, (G + 1) * 512)
                    hT = hT2[G % 2]
                    pT_ = pT2[G % 2]
                    if G + 1 < NG:
                        d_loads(G + 1)

                    def norm_chain(tt):
                        s_ = st[tt % 2]
                        a_ = a1[tt % 2]
                        op("act", lambda e: e.activation(out=a_[:], in_=hT[tt][:], func=AF.Square, accum_out=s_[:, 0:1]),
                           reads=[hT[tt].r], writes=[a_.r, s_.r])
                        op("act", lambda e: e.activation(out=s_[:, 1:2], in_=s_[:, 0:1], func=AF.Ln, scale=1.0 / 2048, bias=epsc),
                           reads=[s_.r, cf.r], writes=[s_.r])
                        op("act", lambda e: e.activation(out=s_[:, 2:3], in_=s_[:, 1:2], func=AF.Exp, scale=-0.5),
                           reads=[s_.r], writes=[s_.r])
                        op("dve", lambda e: e.scalar_tensor_tensor(out=a_[:], in0=hT[tt][:], scalar=s_[:, 2:3], in1=gbp[:],
                                                                   op0=ALU.mult, op1=ALU.mult),
                           reads=[hT[tt].r, s_.r, gbp.r], writes=[a_.r])

                    def transposes(tt):
                        a_ = a1[tt % 2]
                        for half in range(2):
                            pb = bank()
                            pbb = pb[:].bitcast(BF16)
                            for c in range(8):
                                cc = half * 8 + c
                                op("pe", lambda e: e.transpose(out=pbb[:, c * 128:(c + 1) * 128], in_=a_[:, cc * 128:(cc + 1) * 128],
                                                               identity=ident), reads=[a_.r, cb.r], writes=[pb.r], inc=(c == 7))
                            dst = a1T[:, half * 8:(half + 1) * 8, tt * 128:(tt + 1) * 128]
                            srcv = pbb.rearrange("p (c t) -> p c t", t=128)
                            if half == 0:
                                op("act", lambda e: e.activation(out=dst, in_=srcv, func=AF.Copy), reads=[pb.r], writes=[a1T.r])
                            else:
                                op("dve", lambda e: e.tensor_copy(out=dst, in_=srcv), reads=[pb.r], writes=[a1T.r])

                    def p_transposes(tt):
                        pb_ = pbf[tt % 2]
                        op("pool", lambda e: e.tensor_copy(out=pb_[:], in_=pT_[tt][:]), reads=[pT_[tt].r], writes=[pb_.r])
                        pb = bank()
                        pbb = pb[:].bitcast(BF16)
                        for c in range(2):
                            op("pe", lambda e: e.transpose(out=pbb[:, c * 128:(c + 1) * 128], in_=pb_[:, c * 128:(c + 1) * 128],
                                                           identity=ident), reads=[pb_.r, cb.r], writes=[pb.r], inc=(c == 1))
                        op("act", lambda e: e.activation(out=ppT[:, :, tt * 128:(tt + 1) * 128],
                                                         in_=pbb[:, 0:256].rearrange("p (c t) -> p c t", t=128), func=AF.Copy),
                           reads=[pb.r], writes=[ppT.r])

                    for cbk in range(4):
                        wt = wq.pop(0)
                        if cbk + 2 < 4:
                            wq.append(loadw(wsc_out, r_w["out"], cbk + 2))
                        else:
                            wq.append(loadw(wsc_pg, r_w["pg"], cbk - 2))
                        for tt in range(4):
                            b = bank()
                            for kc in range(16):
                                op("pe", lambda e: e.matmul(b[:], lhsT=mx[:, kc, tt * 128:(tt + 1) * 128], rhs=wt[:, kc, :],
                                                            start=(kc == 0), stop=(kc == 15)),
                                   reads=[mx.r, wt.r], writes=[b.r], inc=(kc == 15))
                            hs = hT[tt][:, cbk * 512:(cbk + 1) * 512]
                            op("dve", lambda e: e.tensor_tensor(out=hs, in0=b[:], in1=hs, op=ALU.add),
                               reads=[b.r, hT[tt].r], writes=[hT[tt].r])
                            if cbk == 1:
                                p_transposes(tt)
                            if cbk == 3:
                                norm_chain(tt)
                                if tt >= 1:
                                    transposes(tt - 1)
                    if G + 1 < NG:
                        d_load_mx(G + 1)
                    transposes(3)
                    for cbk in range(4):
                        wt = wq.pop(0)
                        if cbk + 2 < 4:
                            wq.append(loadw(wsc_pg, r_w["pg"], cbk + 2))
                        elif G + 1 < NG:
                            wq.append(loadw(wsc_out, r_w["out"], cbk - 2))
                        cs = slice(cbk * 512, (cbk + 1) * 512)
                        for tt in range(4):
                            bg = bank()
                            for kc in range(16):
                                op("pe", lambda e: e.matmul(bg[:], lhsT=a1T[:, kc, tt * 128:(tt + 1) * 128], rhs=wt[:, kc, :],
                                                            start=(kc == 0), stop=(kc == 15)),
                                   reads=[a1T.r, wt.r], writes=[bg.r], inc=(kc == 15))
                            bp = bank()
                            for kc in range(2):
                                op("pe", lambda e: e.matmul(bp[:], lhsT=ppT[:, kc, tt * 128:(tt + 1) * 128], rhs=wpp[:, kc, cs],
                                                            start=(kc == 0), stop=(kc == 1)),
                                   reads=[ppT.r, wpp.r], writes=[bp.r], inc=(kc == 1))
                            th = nft()
                            op("act", lambda e: e.activation(out=th[:], in_=bg[:], func=AF.Exp, scale=-1.0), reads=[bg.r], writes=[th.r])
                            tl = nft()
                            op("act", lambda e: e.activation(out=tl[:], in_=th[:], func=AF.Ln, bias=onec), reads=[th.r, cf.r], writes=[tl.r])
                            t2 = nft()
                            op("act", lambda e: e.activation(out=t2[:], in_=tl[:], func=AF.Exp, scale=-1.0), reads=[tl.r], writes=[t2.r])
                            t3 = nft()
                            op("dve", lambda e: e.tensor_tensor(out=t3[:], in0=bp[:], in1=t2[:], op=ALU.mult),
                               reads=[bp.r, t2.r], writes=[t3.r])
                            hs = hT[tt][:, cs]
                            op("pool", lambda e: e.tensor_tensor(out=hs, in0=hs, in1=t3[:], op=ALU.add),
                               reads=[hT[tt].r, t3.r], writes=[hT[tt].r])
                    for tt in range(4):
                        r0 = G * 512 + tt * 128
                        dma(h_dst[r0:r0 + 128, :], hT[tt][:], reads=[hT[tt].r], writes=[r_hdst])
                fw.barrier()

        for l in range(depth):
            h_src = x_d if l == 0 else h1_d
            h_dst = y_d if l == depth - 1 else h1_d
            phase_a(l, h_src, r_h[l])
            if stop_after == "A":
                break
            phase_b(l)
            if stop_after == "B":
                break
            phase_c(l)
            if stop_after == "C":
                break
            phase_d(l, h_src, r_h[l], h_dst, r_h[l + 1])

        fw.dead = False
        fw.barrier()
        print("instructions:", fw.ninstr, flush=True)
    return nc


def _consts():
    cbm = np.zeros((128, NCB), np.float32)
    cbm[:, CB_ID:CB_ID + 128] = np.eye(128)
    cbm[:, CB_ONES:CB_ONES + 128] = 1.0

    def perm_block(mat, base, R_):
        half = R_ // 2
        for m in range(R_):
            if m < half:
                mat[base + m + half, base + m] = -1.0
            else:
                mat[base + m - half, base + m] = 1.0

    perm_block(cbm[:, CB_PMLA:CB_PMLA + 128], 0, 64)
    perm_block(cbm[:, CB_PDSA:CB_PDSA + 128], 0, 32)
    perm_block(cbm[:, CB_PIDX:CB_PIDX + 128], 0, 16)
    perm_block(cbm[:, CB_PIDX:CB_PIDX + 128], 64, 16)
    kk = np.arange(128)[:, None]
    qq = np.arange(128)[None, :]
    cbm[:, CB_TRI:CB_TRI + 128] = (kk <= qq).astype(np.float32)
    cfm = np.zeros((128, NCF), np.float32)
    cfm[:, CF_IBIAS:CF_IBIAS + 128] = np.where(qq.T >= kk.T, 0.0, -1e30)
    theta = np.float32(500000.0)
    pidx = np.arange(128)
    inv_mla = np.where(pidx < 64, theta ** (-(np.arange(128) % 32).astype(np.float32) / np.float32(32)), 0.0)
    inv_dsa = np.where(pidx < 32, theta ** (-(np.arange(128) % 16).astype(np.float32) / np.float32(16)), 0.0)
    inv_idx = np.where((pidx % 64) < 16, theta ** (-((np.arange(128) % 64) % 8).astype(np.float32) / np.float32(8)), 0.0)
    cfm[:, CF_INV + 0] = inv_mla
    cfm[:, CF_INV + 1] = inv_dsa
    cfm[:, CF_INV + 2] = inv_idx
    cfm[:, CF_EPS] = EPS
    cfm[:, CF_ONE] = 1.0
    cfm[:, CF_ONESF:CF_ONESF + 128] = 1.0
    return cbm.astype(ml_dtypes.bfloat16), cfm.astype(np.float32)


def _layout_small(inputs, depth):
    sp = np.zeros((depth, 128, NSP), np.float32)
    rowp = np.zeros((depth * 2, 2048), np.float32)
    for l in range(depth):
        sp[l, :, SP_GQ:SP_GQ + 4] = np.asarray(inputs["mla_gq"][l]).reshape(4, 128).T
        sp[l, :, SP_GKV:SP_GKV + 4] = np.asarray(inputs["mla_gkv"][l]).reshape(4, 128).T
        qn = np.asarray(inputs["mla_qn"][l])
        kn = np.asarray(inputs["mla_kn"][l])
        sp[l, :, SP_QN_N] = qn[:128]
        sp[l, :64, SP_QN_R] = qn[128:]
        sp[l, :, SP_KN_N] = kn[:128]
        sp[l, :64, SP_KN_R] = kn[128:]
        sp[l, :, SP_DQN] = np.asarray(inputs["dsa_qn"][l])
        sp[l, :, SP_DKN] = np.asarray(inputs["dsa_kn"][l])
        cw = np.asarray(inputs["conv_w"][l])
        for j in range(4):
            for k in range(3):
                sp[l, :, SP_CONV + j * 3 + k] = cw[k, j * 128:(j + 1) * 128]
        rowp[2 * l] = np.asarray(inputs["norm_in"][l])
        rowp[2 * l + 1] = np.asarray(inputs["ple_norm"][l])
    return sp, rowp


def make_in_maps(inputs, S=4096, depth=2, cores=8):
    cbm, cfm = _consts()
    sp, rowp = _layout_small(inputs, depth)
    f = lambda k: np.ascontiguousarray(np.asarray(inputs[k], dtype=np.float32)[:depth])
    shared = {
        "w_in": f("w_in"), "w_uq": f("mla_w_uq"), "w_ukv": f("mla_w_ukv"), "w_out": f("w_out"),
        "w_pg": f("ple_w_gate"), "w_pp": f("ple_w_proj"), "sp": sp, "rowp": rowp, "cb": cbm, "cf": cfm,
    }
    x = np.asarray(inputs["x"], dtype=np.float32)
    p = np.asarray(inputs["p"], dtype=np.float32)
    pos = np.asarray(inputs["positions"]).astype(np.int32)
    maps = []
    for b in range(cores):
        m = dict(shared)
        m["x"] = np.ascontiguousarray(x[b, :S])
        m["p"] = np.ascontiguousarray(p[:depth, b, :S])
        m["pos"] = np.ascontiguousarray(pos[b:b + 1, :S])
        maps.append(m)
    return maps


def kernel(**inputs):
    nc = build()
    maps = make_in_maps(inputs)
    res = run_bass_kernel_spmd(nc, maps, core_ids=list(range(8)))
    return np.stack([r["y"] for r in res.results], axis=0)
```

```python
import math
from contextlib import ExitStack

import numpy as np
import ml_dtypes
import concourse.bass as bass
import concourse.mybir as mybir
from concourse.bass_utils import run_bass_kernel_spmd

F32 = mybir.dt.float32
BF16 = mybir.dt.bfloat16
I32 = mybir.dt.int32
ALU = mybir.AluOpType
AF = mybir.ActivationFunctionType

D_MODEL = 2048
N_IN = 7056
N_SC = 7120
EPS = 1e-6
TOPK = 256
NBISECT = 20
PI = math.pi


class StopBuild(Exception):
    pass


class R:
    __slots__ = ("name", "w", "r")

    def __init__(self, name=""):
        self.name = name
        self.w = None
        self.r = {}


class Tile:
    def __init__(self, t, name="", nres=1):
        self.t = t
        self.rs = [R("%s.%d" % (name, i)) for i in range(nres)]

    def __getitem__(self, idx):
        return self.t[idx]

    @property
    def r(self):
        return self.rs[0]


class FW:
    ENG = ("pe", "act", "dve", "pool", "sp")

    def __init__(self, nc, es, n_dma_sems=8):
        self.nc = nc
        self.eng = {"pe": nc.tensor, "act": nc.scalar, "dve": nc.vector, "pool": nc.gpsimd, "sp": nc.sync}
        self.sem = {}
        self.cnt = {}
        for e in self.ENG:
            self.sem[e] = es.enter_context(nc.semaphore("s_" + e))
            self.cnt[e] = 0
        self.dkeys = []
        for i in range(n_dma_sems):
            k = "d%d" % i
            self.sem[k] = es.enter_context(nc.semaphore("s_" + k))
            self.cnt[k] = 0
            self.dkeys.append(k)
        self.dma_i = 0
        self.waited = {e: {} for e in self.ENG}
        self.ninstr = 0
        self.dead = False

    def _wait(self, e, deps):
        for k, v in deps.items():
            if v > self.cnt[k]:
                raise RuntimeError("dependency on future instruction %s %d > %d" % (k, v, self.cnt[k]))
            if self.waited[e].get(k, 0) < v:
                self.eng[e].wait_ge(self.sem[k], v)
                self.waited[e][k] = v
                self.ninstr += 1

    @staticmethod
    def _add(deps, kv):
        k, v = kv
        if deps.get(k, 0) < v:
            deps[k] = v

    def _deps(self, e, reads, writes):
        deps = {}
        for r in reads:
            if r.w is not None:
                self._add(deps, r.w)
        for w in writes:
            if w.w is not None and not (e == "pe" and w.w[0] == "pe"):
                self._add(deps, w.w)
            for k, v in w.r.items():
                if not (e == "pe" and k == "pe"):
                    self._add(deps, (k, v))
        return deps

    def op(self, e, fn, reads=(), writes=(), inc=True):
        if self.dead:
            return None
        self._wait(e, self._deps(e, reads, writes))
        ins = fn(self.eng[e])
        self.ninstr += 1
        if inc:
            self.cnt[e] += 1
            ins.then_inc(self.sem[e], 1)
            val = self.cnt[e]
        else:
            val = self.cnt[e] + 1
        for r in reads:
            if r.r.get(e, 0) < val:
                r.r[e] = val
        for w in writes:
            w.w = (e, val)
            w.r = {}
        return ins

    def dma(self, out_ap, in_ap, reads=(), writes=(), e="sp"):
        if self.dead:
            return None
        k = self.dkeys[self.dma_i % len(self.dkeys)]
        self.dma_i += 1
        deps = self._deps("__dma__", reads, writes)
        if self.cnt[k] > 0:
            self._add(deps, (k, self.cnt[k]))
        self._wait(e, deps)
        ins = self.eng[e].dma_start(out=out_ap, in_=in_ap)
        self.ninstr += 1
        self.cnt[k] += 16
        ins.then_inc(self.sem[k], 16)
        val = self.cnt[k]
        for r in reads:
            r.r[k] = val
        for w in writes:
            w.w = (k, val)
            w.r = {}
        return ins

    def barrier(self):
        if self.dead:
            return
        for e in self.ENG:
            deps = {}
            for k in list(self.ENG) + self.dkeys:
                if k != e and self.cnt[k] > 0:
                    deps[k] = self.cnt[k]
            self._wait(e, deps)


SC_CQ, SC_CKV, SC_KPE, SC_MLAZ, SC_CONV = 0, 512, 1024, 1088, 1856
SC_DQ, SC_DK, SC_DV, SC_IW, SC_DSAZ, SC_IQ, SC_IK = 3904, 4672, 4928, 5184, 5200, 5968, 6992
PIECES = [
    ("cq", 0, 512), ("ckv", 512, 512), ("kpez", 1024, 448), ("mlaz2", 1472, 384),
    ("conv0", 1856, 512), ("conv1", 2368, 512), ("conv2", 2880, 512), ("conv3", 3392, 512),
    ("dq0", 3904, 512), ("dq1dk", 4416, 512), ("dviw", 4928, 272), ("dsaz0", 5200, 512),
    ("dsaz1iq", 5712, 512), ("iq1", 6224, 512), ("iq2ik", 6736, 384),
]
SP_GQ, SP_GKV, SP_QN_N, SP_QN_R, SP_KN_N, SP_KN_R, SP_DQN, SP_DKN, SP_CONV, NSP = 0, 4, 8, 9, 10, 11, 12, 13, 14, 26
CB_ID, CB_ONES, CB_PMLA, CB_PDSA, CB_PIDX, CB_TRI, NCB = 0, 128, 256, 384, 512, 640, 768
CF_IBIAS, CF_INV, CF_EPS, CF_ONE, CF_ONESF, NCF = 0, 128, 131, 132, 133, 133 + 128


def build(S=4096, depth=2, debug=False, stop_after=None):
    nc = bass.Bass("TRN2", target_bir_lowering=False)
    NG = S // 512
    NT = S // 128
    kI = "ExternalOutput" if debug else "Internal"

    def dram(name, shape, dt, kind):
        return nc.dram_tensor(name, shape, dt, kind=kind).ap()

    x_d = dram("x", [S, 2048], F32, "ExternalInput")
    p_d = dram("p", [depth, S, 256], F32, "ExternalInput")
    pos_d = dram("pos", [1, S], I32, "ExternalInput")
    w_in_d = dram("w_in", [depth, 2048, N_IN], F32, "ExternalInput")
    w_uq_d = dram("w_uq", [depth, 512, 1152], F32, "ExternalInput")
    w_ukv_d = dram("w_ukv", [depth, 512, 1536], F32, "ExternalInput")
    w_out_d = dram("w_out", [depth, 2048, 2048], F32, "ExternalInput")
    w_pg_d = dram("w_pg", [depth, 2048, 2048], F32, "ExternalInput")
    w_pp_d = dram("w_pp", [depth, 256, 2048], F32, "ExternalInput")
    sp_d = dram("sp", [depth, 128, NSP], F32, "ExternalInput")
    rowp_d = dram("rowp", [depth * 2, 2048], F32, "ExternalInput")
    cb_d = dram("cb", [128, NCB], BF16, "ExternalInput")
    cf_d = dram("cf", [128, NCF], F32, "ExternalInput")
    y_d = dram("y", [S, 2048], F32, "ExternalOutput")

    tab_d = dram("tab", [6, 128, S], F32, kI)
    wsc_in_l = [dram("wsc_in%d" % i, [128, 16, N_SC], BF16, "Internal") for i in range(depth)]
    wsc_uq_l = [dram("wsc_uq%d" % i, [128, 4, 1152], BF16, "Internal") for i in range(depth)]
    wsc_ukv_l = [dram("wsc_ukv%d" % i, [128, 4, 1536], BF16, "Internal") for i in range(depth)]
    wsc_out_l = [dram("wsc_out%d" % i, [128, 16, 2048], BF16, "Internal") for i in range(depth)]
    wsc_pg_l = [dram("wsc_pg%d" % i, [128, 16, 2048], BF16, "Internal") for i in range(depth)]
    wsc_pp_l = [dram("wsc_pp%d" % i, [128, 2, 2048], BF16, "Internal") for i in range(depth)]
    h1_d = dram("h1", [S, 2048], F32, kI)
    qn_d = dram("qn", [6, 128, S], BF16, kI)
    qr_d = dram("qr", [6, 128, S], BF16, kI)
    kn_d = dram("kn", [6, 128, S], BF16, kI)
    kr_d = dram("kr", [6, 128, S], BF16, kI)
    vm_d = dram("vm", [6, 128, NT, 128], BF16, kI)
    gm_d = dram("gm", [768, S], F32, kI)
    dq_d = dram("dq", [6, 128, S], BF16, kI)
    dk_d = dram("dk", [2, 128, S], BF16, kI)
    dv_d = dram("dv", [2, 128, NT, 128], BF16, kI)
    gd_d = dram("gd", [768, S], F32, kI)
    iq_d = dram("iq", [8, 128, S], BF16, kI)
    ik_d = dram("ik", [128, S], BF16, kI)
    iw_d = dram("iw", [S, 32], F32, kI)
    mix_d = dram("mix", [2048, S], BF16, kI)

    es_top = ExitStack()
    with es_top:
        fw = FW(nc, es_top)
        op, dma = fw.op, fw.dma

        sb_uid = [0]

        def ck(name):
            if stop_after == name:
                fw.dead = True

        def sb(es, name, shape, dt, nres=1):
            sb_uid[0] += 1
            nm = "sb%d_%s" % (sb_uid[0], name)
            return Tile(es.enter_context(nc.sbuf_tensor(nm, shape, dt)), nm, nres)

        banks = [Tile(es_top.enter_context(nc.psum_tensor("bank%d" % i, [128, 512], F32)), "bank%d" % i)
                 for i in range(8)]
        bank_i = [0]

        def bank():
            b = banks[bank_i[0] % 8]
            bank_i[0] += 1
            return b

        cb = sb(es_top, "cb", [128, NCB], BF16)
        cf = sb(es_top, "cf", [128, NCF], F32)
        dma(cb[:], cb_d[:, :], writes=[cb.r])
        dma(cf[:], cf_d[:, :], writes=[cf.r])
        ident = cb[:, CB_ID:CB_ID + 128]
        ones = cb[:, CB_ONES:CB_ONES + 128]
        epsc = cf[:, CF_EPS:CF_EPS + 1]
        onec = cf[:, CF_ONE:CF_ONE + 1]

        r_tab = R("tab")
        r_w_l = [{n: R("wsc_%s%d" % (n, i)) for n in ("in", "uq", "ukv", "out", "pg", "pp")} for i in range(depth)]
        r_h = [R("h%d" % i) for i in range(depth + 1)]
        r_A = R("Aout")
        r_mix = R("mix")

        def cast_units(l):
            wsc_in, wsc_uq, wsc_ukv = wsc_in_l[l], wsc_uq_l[l], wsc_ukv_l[l]
            wsc_out, wsc_pg, wsc_pp = wsc_out_l[l], wsc_pg_l[l], wsc_pp_l[l]
            rw = r_w_l[l]
            units = []

            def mk(src_ap, n, stores, r):
                def ld(stg):
                    dma(stg[:, 0:n], src_ap, writes=[stg.r])

                def u(stg, stb, ce):
                    if ce == "act":
                        op("act", lambda e: e.activation(out=stb[:, 0:n], in_=stg[:, 0:n], func=AF.Copy),
                           reads=[stg.r], writes=[stb.r])
                    else:
                        op(ce, lambda e: e.tensor_copy(out=stb[:, 0:n], in_=stg[:, 0:n]), reads=[stg.r], writes=[stb.r])
                    for f in stores:
                        d_ap, s_ap = f(stb)
                        dma(d_ap, s_ap, reads=[stb.r], writes=[r])
                units.append((ld, u))

            for kc in range(16):
                src = w_in_d[l, kc * 128:(kc + 1) * 128, :]
                mk(src[:, 0:1856], 1856, [lambda t, kc=kc: (wsc_in[:, kc, 0:1856], t[:, 0:1856])], rw["in"])
                mk(src[:, 1856:3904], 2048,
                   [(lambda t, kind=kind, kc=kc: (
                       wsc_in[:, kc, 1856:3904].rearrange("p (j k c) -> p j k c", k=4, c=128)[:, :, kind, :],
                       t[:, kind * 512:(kind + 1) * 512].rearrange("p (j c) -> p j c", c=128)))
                    for kind in range(4)], rw["in"])
                mk(src[:, 3904:5184], 1280, [lambda t, kc=kc: (wsc_in[:, kc, 3904:5184], t[:, 0:1280])], rw["in"])
                mk(src[:, 5184:7056], 1872,
                   [lambda t, kc=kc: (wsc_in[:, kc, SC_DSAZ:SC_DSAZ + 1792], t[:, 0:1792]),
                    lambda t, kc=kc: (wsc_in[:, kc, SC_IW:SC_IW + 16], t[:, 1792:1808]),
                    lambda t, kc=kc: (wsc_in[:, kc, SC_IK:SC_IK + 64], t[:, 1808:1872]),
                    lambda t, kc=kc: (wsc_in[:, kc, SC_IK + 64:SC_IK + 128], t[:, 1808:1872])], rw["in"])
            for kc in range(4):
                mk(w_uq_d[l, kc * 128:(kc + 1) * 128, :], 1152, [lambda t, kc=kc: (wsc_uq[:, kc, :], t[:, 0:1152])], rw["uq"])
                mk(w_ukv_d[l, kc * 128:(kc + 1) * 128, :], 1536,
                   [(lambda t, two=two, kc=kc: (
                       wsc_ukv[:, kc, two * 768:(two + 1) * 768].rearrange("p (h c) -> p h c", c=128),
                       t[:, 0:1536].rearrange("p (h two c) -> p h two c", two=2, c=128)[:, :, two, :]))
                    for two in range(2)], rw["ukv"])
            for kc in range(16):
                mk(w_out_d[l, kc * 128:(kc + 1) * 128, :], 2048, [lambda t, kc=kc: (wsc_out[:, kc, :], t[:, :])], rw["out"])
                mk(w_pg_d[l, kc * 128:(kc + 1) * 128, :], 2048, [lambda t, kc=kc: (wsc_pg[:, kc, :], t[:, :])], rw["pg"])
            for kc in range(2):
                mk(w_pp_d[l, kc * 128:(kc + 1) * 128, :], 2048, [lambda t, kc=kc: (wsc_pp[:, kc, :], t[:, :])], rw["pp"])
            return units

        with ExitStack() as es:
            NB_ = 6
            stg = [sb(es, "wstg%d" % i, [128, 2048], F32) for i in range(NB_)]
            stb = [sb(es, "wstb%d" % i, [128, 2048], BF16) for i in range(NB_)]
            posi = sb(es, "posi", [128, S], I32)
            posf = sb(es, "posf", [128, S], F32)
            ang = sb(es, "ang", [128, S], F32)
            ti = sb(es, "ti", [128, S], I32)
            tf = sb(es, "tf", [128, S], F32)
            rr = sb(es, "rr", [128, S], F32)
            mm = sb(es, "mm", [128, S], F32)
            tout = sb(es, "tout", [128, S], F32)

            def setup_gen():
                dma(posi[:], pos_d.partition_broadcast(128).rearrange("p o s -> p (o s)"), writes=[posi.r])
                op("dve", lambda e: e.tensor_copy(out=posf[:], in_=posi[:]), reads=[posi.r], writes=[posf.r])
                yield
                C1 = 6.28125
                C2 = 2.0 * PI - C1
                for typ in range(3):
                    for cs in range(2):
                        shift = 0.0 if cs == 0 else PI / 2.0
                        op("dve", lambda e: e.tensor_scalar(out=ang[:], in0=posf[:], scalar1=cf[:, CF_INV + typ:CF_INV + typ + 1],
                                                            scalar2=shift, op0=ALU.mult, op1=ALU.add),
                           reads=[posf.r, cf.r], writes=[ang.r])
                        yield
                        op("dve", lambda e: e.tensor_scalar(out=ti[:], in0=ang[:], scalar1=1.0 / (2.0 * PI), scalar2=None,
                                                            op0=ALU.mult), reads=[ang.r], writes=[ti.r])
                        yield
                        op("dve", lambda e: e.tensor_copy(out=tf[:], in_=ti[:]), reads=[ti.r], writes=[tf.r])
                        yield
                        op("dve", lambda e: e.scalar_tensor_tensor(out=rr[:], in0=tf[:], scalar=-C1, in1=ang[:],
                                                                   op0=ALU.mult, op1=ALU.add),
                           reads=[tf.r, ang.r], writes=[rr.r])
                        yield
                        op("dve", lambda e: e.scalar_tensor_tensor(out=ang[:], in0=tf[:], scalar=-C2, in1=rr[:],
                                                                   op0=ALU.mult, op1=ALU.add),
                           reads=[tf.r, rr.r], writes=[ang.r])
                        yield
                        op("dve", lambda e: e.tensor_scalar(out=mm[:], in0=ang[:], scalar1=PI, scalar2=None, op0=ALU.is_gt),
                           reads=[ang.r], writes=[mm.r])
                        yield
                        op("dve", lambda e: e.scalar_tensor_tensor(out=rr[:], in0=mm[:], scalar=-2.0 * PI, in1=ang[:],
                                                                   op0=ALU.mult, op1=ALU.add),
                           reads=[mm.r, ang.r], writes=[rr.r])
                        yield
                        op("dve", lambda e: e.tensor_scalar(out=mm[:], in0=rr[:], scalar1=-PI, scalar2=None, op0=ALU.is_lt),
                           reads=[rr.r], writes=[mm.r])
                        yield
                        op("dve", lambda e: e.scalar_tensor_tensor(out=ang[:], in0=mm[:], scalar=2.0 * PI, in1=rr[:],
                                                                   op0=ALU.mult, op1=ALU.add),
                           reads=[mm.r, rr.r], writes=[ang.r])
                        yield
                        op("dve", lambda e: e.tensor_scalar(out=rr[:], in0=ang[:], scalar1=3.141592, scalar2=-3.141592,
                                                            op0=ALU.min, op1=ALU.max), reads=[ang.r], writes=[rr.r])
                        yield
                        op("act", lambda e: e.activation(out=tout[:], in_=rr[:], func=AF.Sin), reads=[rr.r], writes=[tout.r])
                        dma(tab_d[typ * 2 + cs], tout[:], reads=[tout.r], writes=[r_tab])
                        yield

            sg = setup_gen()
            cu0 = cast_units(0)
            PF = 4
            for i in range(min(PF, len(cu0))):
                cu0[i][0](stg[i % NB_])
            for i in range(len(cu0)):
                if i + PF < len(cu0):
                    cu0[i + PF][0](stg[(i + PF) % NB_])
                cu0[i][1](stg[i % NB_], stb[i % NB_], ("pool", "act", "act")[i % 3])
                next(sg, None)
            for _ in sg:
                pass
            fw.barrier()
        if stop_after == "setup":
            fw.barrier()
            return nc

        def phase_a(l, h_src, r_hsrc):
            wsc_in, wsc_uq, wsc_ukv = wsc_in_l[l], wsc_uq_l[l], wsc_ukv_l[l]
            r_w = r_w_l[l]
            with ExitStack() as es:
                spt = sb(es, "spt", [128, NSP], F32)
                gb = sb(es, "gb", [128, 2048], F32)
                wuq = sb(es, "wuq", [128, 4, 1152], BF16)
                wukv = sb(es, "wukv", [128, 4, 1536], BF16)
                aT = sb(es, "aT", [128, 16, 512], BF16)
                htile = [sb(es, "ht%d" % i, [128, 2048], F32) for i in range(2)]
                an = [sb(es, "an%d" % i, [128, 2048], BF16) for i in range(4)]
                st = [sb(es, "st%d" % i, [128, 4], F32) for i in range(2)]
                wbuf = [sb(es, "wb%d" % i, [128, 16, 512], BF16) for i in range(3)]
                tabs = [sb(es, "tab%d" % i, [128, 512], F32) for i in range(6)]
                cqn = sb(es, "cqn", [128, 4, 512], BF16)
                ckvn = sb(es, "ckvn", [128, 4, 512], BF16)
                sq = [sb(es, "sq%d" % i, [128, 512], BF16) for i in range(6)]
                sq64 = [sb(es, "sq64_%d" % i, [128, 512], BF16) for i in range(2)]
                sqkpe_t = sb(es, "sqkpe", [128, 512], BF16)
                f32t = [sb(es, "f32t%d" % i, [128, 512], F32) for i in range(12)]
                b16t = [sb(es, "b16t%d" % i, [128, 512], BF16) for i in range(8)]
                pad64 = [sb(es, "pad64_%d" % i, [128, 512], BF16) for i in range(3)]
                krot = sb(es, "krot", [128, 512], F32)
                uconv = [sb(es, "uconv%d" % j, [128, 516], F32) for j in range(4)]
                tmo = [sb(es, "tmo%d" % i, [128, 768], BF16) for i in range(2)]
                iwt = [sb(es, "iwt%d" % i, [128, 32], F32) for i in range(2)]
                rot = {"sq": 0, "sq64": 0, "f": 0, "b": 0, "pad": 0, "w": 0, "tmo": 0, "iwt": 0}

                def nxt(lst, key):
                    t = lst[rot[key] % len(lst)]
                    rot[key] += 1
                    return t

                dma(spt[:], sp_d[l], writes=[spt.r])
                dma(gb[:], rowp_d[2 * l:2 * l + 1, :].partition_broadcast(128).rearrange("p o s -> p (o s)"), writes=[gb.r])
                dma(wuq[:], wsc_uq[:, :, :], reads=[r_w["uq"]], writes=[wuq.r])
                dma(wukv[:], wsc_ukv[:, :, :], reads=[r_w["ukv"]], writes=[wukv.r])
                for t_ in sq64 + pad64 + [sqkpe_t]:
                    op("pool", lambda e: e.memset(t_[:], 0.0), writes=[t_.r])
                op("pool", lambda e: e.memset(krot[:], 0.0), writes=[krot.r])
                for j in range(4):
                    op("pool", lambda e: e.memset(uconv[j][:], 0.0), writes=[uconv[j].r])

                ck("A0")

                def col(c):
                    return spt[:, c:c + 1]

                def load_piece(pi):
                    name, c0, n = PIECES[pi]
                    wt = nxt(wbuf, "w")
                    dma(wt[:, :, 0:n], wsc_in[:, :, c0:c0 + n], reads=[r_w["in"]], writes=[wt.r])
                    return wt

                def fm_mm(wt, off, M, src=None, wsrc=None, nk=16):
                    b = bank()
                    a = aT if src is None else src
                    for kc in range(nk):
                        op("pe", lambda e: e.matmul(b[0:M, :], lhsT=wt[:, kc, off:off + M], rhs=a[:, kc, :],
                                                    start=(kc == 0), stop=(kc == nk - 1)),
                           reads=[wt.r, a.r], writes=[b.r], inc=(kc == nk - 1))
                    return b

                def square(b, M):
                    if M == 128:
                        s = nxt(sq, "sq")
                        op("act", lambda e: e.activation(out=s[:], in_=b[:], func=AF.Square), reads=[b.r], writes=[s.r])
                    else:
                        s = nxt(sq64, "sq64")
                        op("act", lambda e: e.activation(out=s[0:M, :], in_=b[0:M, :], func=AF.Square),
                           reads=[b.r], writes=[s.r])
                    return s

                def rstd_from(sqs, nfeat):
                    sb_ = bank()
                    for i, s in enumerate(sqs):
                        op("pe", lambda e: e.matmul(sb_[:], lhsT=ones, rhs=s[:], start=(i == 0), stop=(i == len(sqs) - 1)),
                           reads=[s.r, cb.r], writes=[sb_.r], inc=(i == len(sqs) - 1))
                    t = nxt(f32t, "f")
                    op("act", lambda e: e.activation(out=t[:], in_=sb_[:], func=AF.Ln, scale=1.0 / nfeat, bias=epsc),
                       reads=[sb_.r, cf.r], writes=[t.r])
                    r_ = nxt(f32t, "f")
                    op("act", lambda e: e.activation(out=r_[:], in_=t[:], func=AF.Exp, scale=-0.5),
                       reads=[t.r], writes=[r_.r])
                    return r_

                def scale_out(b, M, gcol, rstd, out_ap, out_r):
                    op("dve", lambda e: e.scalar_tensor_tensor(out=out_ap, in0=b[0:M, :], scalar=col(gcol)[0:M, :],
                                                               in1=rstd[0:M, :], op0=ALU.mult, op1=ALU.mult),
                       reads=[b.r, spt.r, rstd.r], writes=[out_r])

                def rope_a(xf):
                    xb = nxt(b16t, "b")
                    op("act", lambda e: e.activation(out=xb[:], in_=xf[:], func=AF.Copy), reads=[xf.r], writes=[xb.r])
                    return xb

                def rope_b(xf, xb, typ, pmat_col, out_t, M=128):
                    pb = bank()
                    op("pe", lambda e: e.matmul(pb[:], lhsT=cb[:, pmat_col:pmat_col + 128], rhs=xb[:], start=True, stop=True),
                       reads=[cb.r, xb.r], writes=[pb.r])
                    t1 = nxt(f32t, "f")
                    op("pool", lambda e: e.tensor_tensor(out=t1[0:M, :], in0=xf[0:M, :], in1=tabs[typ * 2 + 1][0:M, :], op=ALU.mult),
                       reads=[xf.r, tabs[typ * 2 + 1].r], writes=[t1.r])
                    t2 = nxt(f32t, "f")
                    op("dve", lambda e: e.tensor_tensor(out=t2[0:M, :], in0=pb[0:M, :], in1=tabs[typ * 2][0:M, :], op=ALU.mult),
                       reads=[pb.r, tabs[typ * 2].r], writes=[t2.r])
                    op("pool", lambda e: e.tensor_tensor(out=out_t[0:M, :], in0=t1[0:M, :], in1=t2[0:M, :], op=ALU.add),
                       reads=[t1.r, t2.r], writes=[out_t.r])

                def rope(xf, typ, pmat_col, out_t, M=128):
                    rope_b(xf, rope_a(xf), typ, pmat_col, out_t, M)

                def pipe3(jobs):
                    n = len(jobs)
                    stt_ = [None] * n
                    for t in range(n + 2):
                        if t < n:
                            stt_[t] = jobs[t][0]()
                        if 0 <= t - 1 < n and jobs[t - 1][1] is not None:
                            jobs[t - 1][1](stt_[t - 1])
                        if 0 <= t - 2 < n and jobs[t - 2][2] is not None:
                            jobs[t - 2][2](stt_[t - 2])

                def silu_from(b, out_t):
                    e1 = nxt(f32t, "f")
                    op("act", lambda e: e.activation(out=e1[:], in_=b[:], func=AF.Exp, scale=-1.0), reads=[b.r], writes=[e1.r])
                    e2 = nxt(f32t, "f")
                    op("act", lambda e: e.activation(out=e2[:], in_=e1[:], func=AF.Ln, bias=onec), reads=[e1.r, cf.r], writes=[e2.r])
                    e3 = nxt(f32t, "f")
                    op("act", lambda e: e.activation(out=e3[:], in_=e2[:], func=AF.Exp, scale=-1.0), reads=[e2.r], writes=[e3.r])
                    op("dve", lambda e: e.tensor_tensor(out=out_t[:], in0=b[:], in1=e3[:], op=ALU.mult),
                       reads=[b.r, e3.r], writes=[out_t.r])

                def a1_norm(G):
                    for tt in range(4):
                        row0 = G * 512 + tt * 128
                        ht = htile[tt % 2]
                        dma(ht[:], h_src[row0:row0 + 128, :], reads=[r_hsrc], writes=[ht.r])
                        s_ = st[tt % 2]
                        a_ = an[tt]
                        op("act", lambda e: e.activation(out=a_[:], in_=ht[:], func=AF.Square, accum_out=s_[:, 0:1]),
                           reads=[ht.r], writes=[a_.r, s_.r])
                        op("act", lambda e: e.activation(out=s_[:, 1:2], in_=s_[:, 0:1], func=AF.Ln, scale=1.0 / 2048, bias=epsc),
                           reads=[s_.r, cf.r], writes=[s_.r])
                        op("act", lambda e: e.activation(out=s_[:, 2:3], in_=s_[:, 1:2], func=AF.Exp, scale=-0.5),
                           reads=[s_.r], writes=[s_.r])
                        op("dve", lambda e: e.scalar_tensor_tensor(out=a_[:], in0=ht[:], scalar=s_[:, 2:3], in1=gb[:],
                                                                   op0=ALU.mult, op1=ALU.mult),
                           reads=[ht.r, s_.r, gb.r], writes=[a_.r])

                def a1_transpose():
                    for tt in range(4):
                        a_ = an[tt]
                        for half in range(2):
                            pb = bank()
                            pbb = pb[:].bitcast(BF16)
                            for c in range(8):
                                cc = half * 8 + c
                                op("pe", lambda e: e.transpose(out=pbb[:, c * 128:(c + 1) * 128], in_=a_[:, cc * 128:(cc + 1) * 128],
                                                               identity=ident),
                                   reads=[a_.r, cb.r], writes=[pb.r], inc=(c == 7))
                            dst = aT[:, half * 8:(half + 1) * 8, tt * 128:(tt + 1) * 128]
                            srcv = pbb.rearrange("p (c t) -> p c t", t=128)
                            if half == 0:
                                op("act", lambda e: e.activation(out=dst, in_=srcv, func=AF.Copy), reads=[pb.r], writes=[aT.r])
                            else:
                                op("dve", lambda e: e.tensor_copy(out=dst, in_=srcv), reads=[pb.r], writes=[aT.r])

                a1_norm(0)
                for G in range(NG):
                    gs = slice(G * 512, (G + 1) * 512)
                    for i in range(6):
                        dma(tabs[i][:], tab_d[i, :, gs], reads=[r_tab], writes=[tabs[i].r])
                    a1_transpose()

                    ck("A1")

                    def store(dst_ap, t, M=128, r=r_A):
                        dma(dst_ap, t[0:M, :], reads=[t.r], writes=[r])

                    wts = {0: load_piece(0), 1: load_piece(1)}

                    def piece(pi):
                        if pi + 2 < len(PIECES):
                            wts[pi + 2] = load_piece(pi + 2)
                        return wts.pop(pi)

                    wt = piece(0)
                    bq = [fm_mm(wt, c * 128, 128) for c in range(4)]
                    sqs = [square(b, 128) for b in bq]
                    wt1 = piece(1)
                    rq = rstd_from(sqs, 512)
                    for c in range(4):
                        scale_out(bq[c], 128, SP_GQ + c, rq, cqn[:, c, :], cqn.r)
                    bkv = [fm_mm(wt1, c * 128, 128) for c in range(4)]
                    sqs = [square(b, 128) for b in bkv]
                    rkv = rstd_from(sqs, 512)
                    for c in range(4):
                        scale_out(bkv[c], 128, SP_GKV + c, rkv, ckvn[:, c, :], ckvn.r)
                    ck("Ackv")
                    wt = piece(2)
                    bkpe = fm_mm(wt, 0, 64)
                    sqkpe = sqkpe_t
                    op("act", lambda e: e.activation(out=sqkpe[0:64, :], in_=bkpe[0:64, :], func=AF.Square),
                       reads=[bkpe.r], writes=[sqkpe.r])
                    kx = nxt(f32t, "f")
                    op("pool", lambda e: e.memset(kx[64:128, :], 0.0), writes=[kx.r])
                    op("dve", lambda e: e.tensor_scalar(out=kx[0:64, :], in0=bkpe[0:64, :], scalar1=col(SP_KN_R)[0:64, :],
                                                        scalar2=None, op0=ALU.mult), reads=[bkpe.r, spt.r], writes=[kx.r])
                    rope(kx, 0, CB_PMLA, krot, M=64)

                    def gate_chunks(wt, off0, n, dst, row0):
                        for c in range(n):
                            b = fm_mm(wt, off0 + c * 128, 128)
                            o = nxt(f32t, "f")
                            silu_from(b, o)
                            store(dst[row0 + c * 128:row0 + (c + 1) * 128, gs], o)

                    gate_chunks(wt, 64, 3, gm_d, 0)
                    ck("Akpe")
                    def q_job(h):
                        def s1():
                            return {"bn": fm_mm(wuq, h * 192, 128, src=cqn, nk=4),
                                    "br": fm_mm(wuq, h * 192 + 128, 64, src=cqn, nk=4)}

                        def s2(d):
                            sa = square(d["bn"], 128)
                            sb_ = square(d["br"], 64)
                            r_ = rstd_from([sa, sb_], 192)
                            o = nxt(b16t, "b")
                            scale_out(d["bn"], 128, SP_QN_N, r_, o[:], o.r)
                            store(qn_d[h, :, gs], o)
                            xf = nxt(f32t, "f")
                            op("pool", lambda e: e.memset(xf[64:128, :], 0.0), writes=[xf.r])
                            scale_out(d["br"], 64, SP_QN_R, r_, xf[0:64, :], xf.r)
                            d["xf"] = xf
                            d["xb"] = rope_a(xf)

                        def s3(d):
                            o2 = nxt(pad64, "pad")
                            rope_b(d["xf"], d["xb"], 0, CB_PMLA, o2, M=64)
                            store(qr_d[h, :, gs], o2)
                        return (s1, s2, s3)

                    pipe3([q_job(h) for h in range(6)])
                    wt = piece(3)
                    gate_chunks(wt, 0, 3, gm_d, 384)
                    ck("Aq")
                    def k_job(h):
                        def s1():
                            return {"bn": fm_mm(wukv, h * 128, 128, src=ckvn, nk=4)}

                        def s2(d):
                            sa = square(d["bn"], 128)
                            r_ = rstd_from([sa, sqkpe], 192)
                            o = nxt(b16t, "b")
                            scale_out(d["bn"], 128, SP_KN_N, r_, o[:], o.r)
                            store(kn_d[h, :, gs], o)
                            o2 = nxt(pad64, "pad")
                            op("pool", lambda e: e.tensor_tensor(out=o2[0:64, :], in0=krot[0:64, :], in1=r_[0:64, :], op=ALU.mult),
                               reads=[krot.r, r_.r], writes=[o2.r])
                            store(kr_d[h, :, gs], o2)
                        return (s1, s2, None)

                    pipe3([k_job(h) for h in range(6)])
                    for tt in range(4):
                        b1 = bank()
                        b2 = bank()
                        for kc in range(4):
                            op("pe", lambda e: e.matmul(b1[:], lhsT=ckvn[:, kc, tt * 128:(tt + 1) * 128], rhs=wukv[:, kc, 768:1280],
                                                        start=(kc == 0), stop=(kc == 3)),
                               reads=[ckvn.r, wukv.r], writes=[b1.r], inc=(kc == 3))
                        for kc in range(4):
                            op("pe", lambda e: e.matmul(b2[:, 0:256], lhsT=ckvn[:, kc, tt * 128:(tt + 1) * 128], rhs=wukv[:, kc, 1280:1536],
                                                        start=(kc == 0), stop=(kc == 3)),
                               reads=[ckvn.r, wukv.r], writes=[b2.r], inc=(kc == 3))
                        o = nxt(tmo, "tmo")
                        op("act", lambda e: e.activation(out=o[:, 0:512], in_=b1[:], func=AF.Copy), reads=[b1.r], writes=[o.r])
                        op("dve", lambda e: e.tensor_copy(out=o[:, 512:768], in_=b2[:, 0:256]), reads=[b2.r], writes=[o.r])
                        dma(vm_d[:, :, G * 4 + tt, :].rearrange("h p d -> p h d"),
                            o[:].rearrange("p (h d) -> p h d", d=128), reads=[o.r], writes=[r_A])
                    ck("Ak")
                    if G + 1 < NG:
                        a1_norm(G + 1)
                    for j in range(4):
                        wt = piece(4 + j)
                        bx = fm_mm(wt, 0, 128)
                        bb_ = fm_mm(wt, 128, 128)
                        bc = fm_mm(wt, 256, 128)
                        bz = fm_mm(wt, 384, 128)
                        U = uconv[j]
                        cxs = nxt(f32t, "f")
                        op("act", lambda e: e.activation(out=cxs[:], in_=bx[:], func=AF.Copy), reads=[bx.r], writes=[cxs.r])
                        op("dve", lambda e: e.tensor_tensor(out=U[:, 4:516], in0=bc[:], in1=cxs[:], op=ALU.mult),
                           reads=[bc.r, cxs.r], writes=[U.r])
                        cv = nxt(f32t, "f")
                        wc = SP_CONV + j * 3
                        op("dve", lambda e: e.tensor_scalar(out=cv[:], in0=U[:, 4:516], scalar1=col(wc + 2), scalar2=None, op0=ALU.mult),
                           reads=[U.r, spt.r], writes=[cv.r])
                        cv2 = nxt(f32t, "f")
                        op("dve", lambda e: e.scalar_tensor_tensor(out=cv2[:], in0=U[:, 3:515], scalar=col(wc + 1), in1=cv[:],
                                                                   op0=ALU.mult, op1=ALU.add), reads=[U.r, spt.r, cv.r], writes=[cv2.r])
                        cv3 = nxt(f32t, "f")
                        op("dve", lambda e: e.scalar_tensor_tensor(out=cv3[:], in0=U[:, 2:514], scalar=col(wc), in1=cv2[:],
                                                                   op0=ALU.mult, op1=ALU.add), reads=[U.r, spt.r, cv2.r], writes=[cv3.r])
                        hal = nxt(f32t, "f")
                        op("pool", lambda e: e.tensor_copy(out=hal[:, 0:2], in_=U[:, 514:516]), reads=[U.r], writes=[hal.r])
                        op("pool", lambda e: e.tensor_copy(out=U[:, 2:4], in_=hal[:, 0:2]), reads=[hal.r], writes=[U.r])
                        sz = nxt(f32t, "f")
                        silu_from(bz, sz)
                        t3 = nxt(f32t, "f")
                        op("dve", lambda e: e.tensor_tensor(out=t3[:], in0=bb_[:], in1=cv3[:], op=ALU.mult),
                           reads=[bb_.r, cv3.r], writes=[t3.r])
                        o = nxt(b16t, "b")
                        op("pool", lambda e: e.tensor_tensor(out=o[:], in0=t3[:], in1=sz[:], op=ALU.mult),
                           reads=[t3.r, sz.r], writes=[o.r])
                        store(mix_d[768 + j * 128:768 + (j + 1) * 128, gs], o, r=r_mix)
                    ck("Aconv")
                    wcache = {}

                    def getw(pi):
                        if pi not in wcache:
                            wcache[pi] = piece(pi)
                        return wcache[pi]

                    def dsa_job(pi, off, gcol, dst):
                        def s1():
                            return {"b": fm_mm(getw(pi), off, 128)}

                        def s2(d):
                            sa = square(d["b"], 128)
                            r_ = rstd_from([sa], 128)
                            xf = nxt(f32t, "f")
                            scale_out(d["b"], 128, gcol, r_, xf[:], xf.r)
                            d["xf"] = xf
                            d["xb"] = rope_a(xf)

                        def s3(d):
                            o = nxt(b16t, "b")
                            rope_b(d["xf"], d["xb"], 1, CB_PDSA, o)
                            store(dst, o)
                        return (s1, s2, s3)

                    jobs = [dsa_job(8, h * 128, SP_DQN, dq_d[h, :, gs]) for h in range(4)]
                    jobs += [dsa_job(9, h * 128, SP_DQN, dq_d[4 + h, :, gs]) for h in range(2)]
                    jobs += [dsa_job(9, 256 + g * 128, SP_DKN, dk_d[g, :, gs]) for g in range(2)]
                    pipe3(jobs)
                    ck("Adq")
                    wt = piece(10)
                    for tt in range(4):
                        b = bank()
                        for kc in range(16):
                            op("pe", lambda e: e.matmul(b[:, 0:272], lhsT=aT[:, kc, tt * 128:(tt + 1) * 128], rhs=wt[:, kc, 0:272],
                                                        start=(kc == 0), stop=(kc == 15)),
                               reads=[aT.r, wt.r], writes=[b.r], inc=(kc == 15))
                        o = nxt(tmo, "tmo")
                        op("act", lambda e: e.activation(out=o[:, 0:256], in_=b[:, 0:256], func=AF.Copy), reads=[b.r], writes=[o.r])
                        row0 = G * 512 + tt * 128
                        dma(dv_d[:, :, G * 4 + tt, :].rearrange("h p d -> p h d"),
                            o[:, 0:256].rearrange("p (h d) -> p h d", d=128), reads=[o.r], writes=[r_A])
                        w_ = nxt(iwt, "iwt")
                        op("act", lambda e: e.activation(out=w_[:, 0:16], in_=b[:, 256:272], func=AF.Abs, scale=0.25),
                           reads=[b.r], writes=[w_.r])
                        op("act", lambda e: e.activation(out=w_[:, 16:32], in_=b[:, 256:272], func=AF.Sign),
                           reads=[b.r], writes=[w_.r])
                        dma(iw_d[row0:row0 + 128, :], w_[:], reads=[w_.r], writes=[r_A])
                    ck("Adv")
                    wt = piece(11)
                    gate_chunks(wt, 0, 4, gd_d, 0)
                    wt = piece(12)
                    gate_chunks(wt, 0, 2, gd_d, 512)

                    wcache[12] = wt

                    def idx_job(pi, off, scale, dst):
                        def s1():
                            return {"b": fm_mm(getw(pi), off, 128)}

                        def s2(d):
                            xf = nxt(f32t, "f")
                            op("act", lambda e: e.activation(out=xf[:], in_=d["b"][:], func=AF.Identity, scale=scale),
                               reads=[d["b"].r], writes=[xf.r])
                            d["xf"] = xf
                            d["xb"] = rope_a(xf)

                        def s3(d):
                            o = nxt(b16t, "b")
                            rope_b(d["xf"], d["xb"], 2, CB_PIDX, o)
                            store(dst, o)
                        return (s1, s2, s3)

                    jobs = [idx_job(12, 256 + c * 128, 0.125, iq_d[c, :, gs]) for c in range(2)]
                    jobs += [idx_job(13, c * 128, 0.125, iq_d[2 + c, :, gs]) for c in range(4)]
                    jobs += [idx_job(14, c * 128, 0.125, iq_d[6 + c, :, gs]) for c in range(2)]
                    jobs += [idx_job(14, 256, 1.0, ik_d[:, gs])]
                    pipe3(jobs)
                fw.barrier()


        def attention(es, nheads, load_head, scale, masker, gate_d, mix_row0, pfx, den_dve=False):
            pt = [sb(es, pfx + "pt%d" % i, [128, 512], BF16) for i in range(4)]
            gt = [sb(es, pfx + "gt%d" % i, [128, 512], F32) for i in range(2)]
            rd = [sb(es, pfx + "rd%d" % i, [128, 512], F32) for i in range(2)]
            of = [sb(es, pfx + "of%d" % i, [128, 512], F32) for i in range(2)]
            ob = [sb(es, pfx + "ob%d" % i, [128, 512], BF16) for i in range(2)]
            dsum = [sb(es, pfx + "ds%d" % i, [128, 512], F32) for i in range(2)] if den_dve else None
            onesf = cf[:, CF_ONESF:CF_ONESF + 128]
            st_i = [0]
            it = [0]

            def run(h, Gq, hd):
                i2 = it[0] % 2
                it[0] += 1
                oacc = banks[4 + 2 * i2]
                dacc = banks[5 + 2 * i2]
                gs = slice(Gq * 512, (Gq + 1) * 512)
                g_ = gt[i2]
                dma(g_[:], gate_d[h * 128:(h + 1) * 128, gs], reads=[r_A], writes=[g_.r])
                nkc = 4 * (Gq + 1)
                qo = hd["qoff"](Gq)

                def qk(kc):
                    q0 = max(kc - 4 * Gq, 0) * 128
                    stb = banks[st_i[0] % 4]
                    st_i[0] += 1
                    two = hd["kr"] is not None
                    op("pe", lambda e: e.matmul(stb[:, q0:512], lhsT=hd["kn"][:, kc * 128:(kc + 1) * 128],
                                                rhs=hd["qn"][:, qo + q0:qo + 512], start=True, stop=not two),
                       reads=[hd["kn"].r, hd["qn"].r], writes=[stb.r], inc=not two)
                    if two:
                        op("pe", lambda e: e.matmul(stb[:, q0:512], lhsT=hd["kr"][:, kc * 128:(kc + 1) * 128],
                                                    rhs=hd["qr"][:, qo + q0:qo + 512], start=False, stop=True),
                           reads=[hd["kr"].r, hd["qr"].r], writes=[stb.r])
                    p = pt[(st_i[0] - 1) % 4]
                    op("act", lambda e: e.activation(out=p[:, q0:512], in_=stb[:, q0:512], func=AF.Exp, scale=scale),
                       reads=[stb.r], writes=[p.r])
                    masker(p, kc, q0, Gq)
                    return p, q0

                def pv(kc, p, q0):
                    op("pe", lambda e: e.matmul(oacc[:, q0:512], lhsT=hd["v"][:, kc, :], rhs=p[:, q0:512],
                                                start=(kc == 0), stop=(kc == nkc - 1)),
                       reads=[hd["v"].r, p.r], writes=[oacc.r], inc=den_dve)
                    if den_dve:
                        ds = dsum[i2]
                        if kc == 0:
                            op("dve", lambda e: e.tensor_copy(out=ds[:], in_=p[:]), reads=[p.r], writes=[ds.r])
                        else:
                            op("dve", lambda e: e.tensor_tensor(out=ds[:, q0:512], in0=ds[:, q0:512], in1=p[:, q0:512], op=ALU.add),
                               reads=[ds.r, p.r], writes=[ds.r])
                        if kc == nkc - 1:
                            op("pe", lambda e: e.matmul(dacc[:], lhsT=onesf, rhs=ds[:], start=True, stop=True),
                               reads=[cf.r, ds.r], writes=[dacc.r])
                    else:
                        op("pe", lambda e: e.matmul(dacc[:, q0:512], lhsT=ones, rhs=p[:, q0:512],
                                                    start=(kc == 0), stop=(kc == nkc - 1)),
                           reads=[cb.r, p.r], writes=[dacc.r])

                pend = [qk(0)]
                for kc in range(nkc):
                    if kc + 1 < nkc:
                        pend.append(qk(kc + 1))
                    p, q0 = pend.pop(0)
                    pv(kc, p, q0)
                r_ = rd[i2]
                op("act", lambda e: e.activation(out=r_[:], in_=dacc[:], func=AF.Ln), reads=[dacc.r], writes=[r_.r])
                op("act", lambda e: e.activation(out=r_[:], in_=r_[:], func=AF.Exp, scale=-1.0), reads=[r_.r], writes=[r_.r])
                o_ = of[i2]
                op("dve", lambda e: e.tensor_tensor(out=o_[:], in0=oacc[:], in1=r_[:], op=ALU.mult),
                   reads=[oacc.r, r_.r], writes=[o_.r])
                b_ = ob[i2]
                op("pool", lambda e: e.tensor_tensor(out=b_[:], in0=o_[:], in1=g_[:], op=ALU.mult),
                   reads=[o_.r, g_.r], writes=[b_.r])
                dma(mix_d[mix_row0 + h * 128:mix_row0 + (h + 1) * 128, gs], b_[:], reads=[b_.r], writes=[r_mix])

            return run

        def phase_b(l):
            with ExitStack() as es:
                cunits = cast_units(l + 1) if l + 1 < depth else []
                NCB_ = 6
                cstg = [sb(es, "b_wstg%d" % i, [128, 2048], F32) for i in range(NCB_)]
                cstb = [sb(es, "b_wstb%d" % i, [128, 2048], BF16) for i in range(3)]
                cu_i = [0]
                ld_i = [0]

                def emit_loads(upto):
                    while ld_i[0] < min(upto, len(cunits)):
                        i = ld_i[0]
                        cunits[i][0](cstg[i % NCB_])
                        ld_i[0] += 1

                def emit_casts(k):
                    for _ in range(k):
                        if cu_i[0] < len(cunits):
                            i = cu_i[0]
                            emit_loads(i + 1)
                            cunits[i][1](cstg[i % NCB_], cstb[i % 3], "dve")
                            cu_i[0] += 1
                    emit_loads(cu_i[0] + 3)

                hb = []
                for i in range(2):
                    hb.append({
                        "kn": sb(es, "b_kn%d" % i, [128, S], BF16), "kr": sb(es, "b_kr%d" % i, [128, S], BF16),
                        "qn": sb(es, "b_qn%d" % i, [128, S], BF16), "qr": sb(es, "b_qr%d" % i, [128, S], BF16),
                        "v": sb(es, "b_v%d" % i, [128, NT, 128], BF16), "qoff": (lambda Gq: Gq * 512)})
                tri = cb[:, CB_TRI:CB_TRI + 128]

                def masker(p, kc, q0, Gq):
                    if kc >= 4 * Gq:
                        op("pool", lambda e: e.tensor_tensor(out=p[:, q0:q0 + 128], in0=p[:, q0:q0 + 128], in1=tri, op=ALU.mult),
                           reads=[p.r, cb.r], writes=[p.r])

                run = attention(es, 6, None, 192.0 ** -0.5, masker, gm_d, 0, "b_", den_dve=False)

                def load(h):
                    d = hb[h % 2]
                    dma(d["kn"][:], kn_d[h], reads=[r_A], writes=[d["kn"].r])
                    dma(d["kr"][:], kr_d[h], reads=[r_A], writes=[d["kr"].r])
                    dma(d["qn"][:], qn_d[h], reads=[r_A], writes=[d["qn"].r])
                    dma(d["qr"][:], qr_d[h], reads=[r_A], writes=[d["qr"].r])
                    dma(d["v"][:], vm_d[h], reads=[r_A], writes=[d["v"].r])
                    return d

                nxt_h = load(0)
                for h in range(6):
                    hd = nxt_h
                    if h + 1 < 6:
                        nxt_h = load(h + 1)
                    for Gq in range(NG):
                        run(h, Gq, hd)
                        emit_casts(3)
                emit_casts(len(cunits))
                fw.barrier()

        def phase_c(l):
            with ExitStack() as es:
                ik2 = sb(es, "c_ik2", [128, S], BF16)
                dk = [sb(es, "c_dk%d" % g, [128, S], BF16) for g in range(2)]
                dv = [sb(es, "c_dv%d" % g, [128, NT, 128], BF16) for g in range(2)]
                iqt = [sb(es, "c_iq%d" % c, [128, 512], BF16) for c in range(8)]
                dqt = [[sb(es, "c_dq%d_%d" % (i, h), [128, 512], BF16) for h in range(6)] for i in range(2)]
                iwt = [sb(es, "c_iw%d" % i, [128, 32], F32) for i in range(4)]
                sc = sb(es, "c_sc", [128, S], F32)
                cj = sb(es, "c_cj", [128, S], BF16)
                mqs = [sb(es, "c_mq%d" % i, [128, S], BF16) for i in range(2)]
                maskTs = [sb(es, "c_maskT%d" % i, [128, NT, 512], BF16) for i in range(2)]
                rt = [sb(es, "c_rt%d" % i, [128, 1024], F32) for i in range(4)]
                bs = [sb(es, "c_bs%d" % i, [128, 8], F32) for i in range(2)]
                wtab = [sb(es, "c_wtab%d" % i, [128, NBISECT + 1], F32) for i in range(2)]
                p2 = sb(es, "c_p2", [128, NBISECT + 1], F32)
                tri = cb[:, CB_TRI:CB_TRI + 128]
                dma(ik2[:], ik_d[:, :], reads=[r_A], writes=[ik2.r])
                for g in range(2):
                    dma(dk[g][:], dk_d[g], reads=[r_A], writes=[dk[g].r])
                    dma(dv[g][:], dv_d[g], reads=[r_A], writes=[dv[g].r])
                for i in range(NBISECT + 1):
                    op("pool", lambda e: e.memset(p2[:, i:i + 1], 2.0 ** -(i + 1)), writes=[p2.r])
                wb_i = [0]

                def wbank():
                    b = banks[wb_i[0] % 4]
                    wb_i[0] += 1
                    return b

                rt_i = [0]

                def idx_part1(qb, qbl):
                    if qb < 2:
                        return
                    qs = slice(qbl * 128, (qbl + 1) * 128)
                    nk = (qb + 1) * 128
                    w_ = iwt[qbl]
                    mq = mqs[qb % 2]
                    for k0 in range(0, nk, 1024):
                        n = min(1024, nk - k0)
                        for hh in range(16):
                            c = hh // 2
                            base = (hh % 2) * 64
                            r_ = rt[rt_i[0] % 4]
                            rt_i[0] += 1
                            for h0 in range(0, n, 512):
                                m = min(512, n - h0)
                                b = wbank()
                                op("pe", lambda e: e.matmul(b[:, 0:m], lhsT=iqt[c][base:base + 64, qs],
                                                            rhs=ik2[base:base + 64, k0 + h0:k0 + h0 + m],
                                                            start=True, stop=True), reads=[iqt[c].r, ik2.r], writes=[b.r])
                                op("act", lambda e: e.activation(out=r_[:, h0:h0 + m], in_=b[:, 0:m], func=AF.Relu, scale=w_[:, hh:hh + 1]),
                                   reads=[b.r, w_.r], writes=[r_.r])
                            if hh == 0:
                                op("dve", lambda e: e.tensor_scalar(out=sc[:, k0:k0 + n], in0=r_[:, 0:n], scalar1=w_[:, 16:17],
                                                                    scalar2=None, op0=ALU.mult), reads=[r_.r, w_.r], writes=[sc.r])
                            else:
                                op("dve", lambda e: e.scalar_tensor_tensor(out=sc[:, k0:k0 + n], in0=r_[:, 0:n], scalar=w_[:, 16 + hh:17 + hh],
                                                                           in1=sc[:, k0:k0 + n], op0=ALU.mult, op1=ALU.add),
                                   reads=[r_.r, w_.r, sc.r], writes=[sc.r])
                    op("dve", lambda e: e.tensor_tensor(out=sc[:, qb * 128:nk], in0=sc[:, qb * 128:nk], in1=cf[:, CF_IBIAS:CF_IBIAS + 128],
                                                        op=ALU.add), reads=[sc.r, cf.r], writes=[sc.r])
                    b_ = bs[qb % 2]
                    wt_ = wtab[qb % 2]
                    op("dve", lambda e: e.tensor_reduce(out=b_[:, 0:1], in_=sc[:, 0:nk], axis=mybir.AxisListType.X, op=ALU.max),
                       reads=[sc.r], writes=[b_.r])
                    op("dve", lambda e: e.tensor_reduce(out=b_[:, 1:2], in_=sc[:, 0:qb * 128], axis=mybir.AxisListType.X, op=ALU.min),
                       reads=[sc.r], writes=[b_.r])
                    op("dve", lambda e: e.tensor_tensor(out=b_[:, 2:3], in0=b_[:, 0:1], in1=b_[:, 1:2], op=ALU.subtract),
                       reads=[b_.r], writes=[b_.r])
                    op("dve", lambda e: e.scalar_tensor_tensor(out=b_[:, 3:4], in0=b_[:, 2:3], scalar=0.5, in1=b_[:, 1:2],
                                                               op0=ALU.mult, op1=ALU.add), reads=[b_.r], writes=[b_.r])
                    op("dve", lambda e: e.tensor_scalar(out=wt_[:], in0=p2[:], scalar1=b_[:, 2:3], scalar2=None, op0=ALU.mult),
                       reads=[p2.r, b_.r], writes=[wt_.r])
                    for i in range(NBISECT):
                        op("dve", lambda e: e.tensor_scalar(out=cj[:, 0:nk], in0=sc[:, 0:nk], scalar1=b_[:, 3:4], scalar2=None,
                                                            op0=ALU.is_ge, op1=ALU.add, accum_out=b_[:, 4:5]),
                           reads=[sc.r, b_.r], writes=[cj.r, b_.r])
                        op("dve", lambda e: e.tensor_scalar(out=b_[:, 5:6], in0=b_[:, 4:5], scalar1=TOPK - 0.5, scalar2=0.5,
                                                            op0=ALU.is_ge, op1=ALU.subtract), reads=[b_.r], writes=[b_.r])
                        op("dve", lambda e: e.scalar_tensor_tensor(out=b_[:, 3:4], in0=b_[:, 5:6], scalar=wt_[:, i:i + 1], in1=b_[:, 3:4],
                                                                   op0=ALU.mult, op1=ALU.add), reads=[b_.r, wt_.r], writes=[b_.r])
                    op("dve", lambda e: e.tensor_tensor(out=b_[:, 6:7], in0=b_[:, 3:4], in1=wt_[:, NBISECT:NBISECT + 1], op=ALU.subtract),
                       reads=[b_.r, wt_.r], writes=[b_.r])
                    op("dve", lambda e: e.tensor_scalar(out=mq[:, 0:nk], in0=sc[:, 0:nk], scalar1=b_[:, 6:7], scalar2=None, op0=ALU.is_ge),
                       reads=[sc.r, b_.r], writes=[mq.r])

                def idx_part2(qb, qbl, maskT):
                    qs = slice(qbl * 128, (qbl + 1) * 128)
                    if qb < 2:
                        for kc in range(qb + 1):
                            src = tri if kc == qb else ones
                            op("pool", lambda e: e.tensor_copy(out=maskT[:, kc, qs], in_=src), reads=[cb.r], writes=[maskT.r])
                        return
                    mq = mqs[qb % 2]
                    for k0 in range(0, qb + 1, 8):
                        n = min(8, qb + 1 - k0)
                        b = wbank()
                        bb_ = b[:].bitcast(BF16)
                        for j in range(n):
                            op("pe", lambda e: e.transpose(out=bb_[:, j * 128:(j + 1) * 128], in_=mq[:, (k0 + j) * 128:(k0 + j + 1) * 128],
                                                           identity=ident), reads=[mq.r, cb.r], writes=[b.r], inc=(j == n - 1))
                        op("act", lambda e: e.activation(out=maskT[:, k0:k0 + n, qs],
                                                         in_=bb_[:, 0:n * 128].rearrange("p (c t) -> p c t", t=128), func=AF.Copy),
                           reads=[b.r], writes=[maskT.r])

                cur_mask = [None]

                def masker(p, kc, q0, Gq):
                    mT = cur_mask[0]
                    op("pool", lambda e: e.tensor_tensor(out=p[:, q0:512], in0=p[:, q0:512], in1=mT[:, kc, q0:512], op=ALU.mult),
                       reads=[p.r, mT.r], writes=[p.r])

                run = attention(es, 6, None, 128.0 ** -0.5, masker, gd_d, 1280, "c_")
                dsa_steps = [[0], [1, 2], [3], [4, 5]]
                for Gq in range(NG + 1):
                    if Gq < NG:
                        gs = slice(Gq * 512, (Gq + 1) * 512)
                        for c in range(8):
                            dma(iqt[c][:], iq_d[c, :, gs], reads=[r_A], writes=[iqt[c].r])
                        for h in range(6):
                            dma(dqt[Gq % 2][h][:], dq_d[h, :, gs], reads=[r_A], writes=[dqt[Gq % 2][h].r])
                        for i in range(4):
                            r0 = Gq * 512 + i * 128
                            dma(iwt[i][:], iw_d[r0:r0 + 128, :], reads=[r_A], writes=[iwt[i].r])
                    for step in range(4):
                        if Gq < NG:
                            idx_part1(Gq * 4 + step, step)
                        if Gq > 0:
                            cur_mask[0] = maskTs[(Gq - 1) % 2]
                            for h in dsa_steps[step]:
                                g = h // 3
                                hd = {"kn": dk[g], "kr": None, "qn": dqt[(Gq - 1) % 2][h], "qr": None, "v": dv[g],
                                      "qoff": (lambda Gq_: 0)}
                                run(h, Gq - 1, hd)
                        if Gq < NG:
                            idx_part2(Gq * 4 + step, step, maskTs[Gq % 2])
                fw.barrier()

        def phase_d(l, h_src, r_hsrc, h_dst, r_hdst):
            wsc_out, wsc_pg, wsc_pp = wsc_out_l[l], wsc_pg_l[l], wsc_pp_l[l]
            r_w = r_w_l[l]
            with ExitStack() as es:
                gbp = sb(es, "d_gbp", [128, 2048], F32)
                wpp = sb(es, "d_wpp", [128, 2, 2048], BF16)
                mx = sb(es, "d_mx", [128, 16, 512], BF16)
                hT2 = [[sb(es, "d_h%d_%d" % (j, i), [128, 2048], F32) for i in range(4)] for j in range(2)]
                pT2 = [[sb(es, "d_p%d_%d" % (j, i), [128, 256], F32) for i in range(4)] for j in range(2)]
                pbf = [sb(es, "d_pbf%d" % i, [128, 256], BF16) for i in range(2)]
                a1 = [sb(es, "d_a1%d" % i, [128, 2048], BF16) for i in range(2)]
                a1T = sb(es, "d_a1T", [128, 16, 512], BF16)
                ppT = sb(es, "d_ppT", [128, 2, 512], BF16)
                wbuf = [sb(es, "d_wb%d" % i, [128, 16, 512], BF16) for i in range(3)]
                st = [sb(es, "d_st%d" % i, [128, 4], F32) for i in range(2)]
                ft = [sb(es, "d_ft%d" % i, [128, 512], F32) for i in range(6)]
                ft_i = [0]
                w_i = [0]

                def nft():
                    t = ft[ft_i[0] % 6]
                    ft_i[0] += 1
                    return t

                dma(gbp[:], rowp_d[2 * l + 1:2 * l + 2, :].partition_broadcast(128).rearrange("p o s -> p (o s)"), writes=[gbp.r])
                dma(wpp[:], wsc_pp[:, :, :], reads=[r_w["pp"]], writes=[wpp.r])

                def loadw(src, rw, cbk):
                    wt = wbuf[w_i[0] % 3]
                    w_i[0] += 1
                    dma(wt[:], src[:, :, cbk * 512:(cbk + 1) * 512], reads=[rw], writes=[wt.r])
                    return wt

                def d_loads(G):
                    for tt in range(4):
                        r0 = G * 512 + tt * 128
                        dma(hT2[G % 2][tt][:], h_src[r0:r0 + 128, :], reads=[r_hsrc], writes=[hT2[G % 2][tt].r])
                        dma(pT2[G % 2][tt][:], p_d[l, r0:r0 + 128, :], writes=[pT2[G % 2][tt].r])

                def d_load_mx(G):
                    dma(mx[:], mix_d[:, G * 512:(G + 1) * 512].rearrange("(c p) s -> p c s", p=128), reads=[r_mix], writes=[mx.r])

                d_load_mx(0)
                d_loads(0)
                wq = [loadw(wsc_out, r_w["out"], 0), loadw(wsc_out, r_w["out"], 1)]
                for G in range(NG):
                    gs = slice(G * 512, (G + 1) * 512)
                    hT = hT2[G % 2]
                    pT_ = pT2[G % 2]
                    if G + 1 < NG:
                        d_loads(G + 1)

                    def norm_chain(tt):
                        s_ = st[tt % 2]
                        a_ = a1[tt % 2]
                        op("act", lambda e: e.activation(out=a_[:], in_=hT[tt][:], func=AF.Square, accum_out=s_[:, 0:1]),
                           reads=[hT[tt].r], writes=[a_.r, s_.r])
                        op("act", lambda e: e.activation(out=s_[:, 1:2], in_=s_[:, 0:1], func=AF.Ln, scale=1.0 / 2048, bias=epsc),
                           reads=[s_.r, cf.r], writes=[s_.r])
                        op("act", lambda e: e.activation(out=s_[:, 2:3], in_=s_[:, 1:2], func=AF.Exp, scale=-0.5),
                           reads=[s_.r], writes=[s_.r])
                        op("dve", lambda e: e.scalar_tensor_tensor(out=a_[:], in0=hT[tt][:], scalar=s_[:, 2:3], in1=gbp[:],
                                                                   op0=ALU.mult, op1=ALU.mult),
                           reads=[hT[tt].r, s_.r, gbp.r], writes=[a_.r])

                    def transposes(tt):
                        a_ = a1[tt % 2]
                        for half in range(2):
                            pb = bank()
                            pbb = pb[:].bitcast(BF16)
                            for c in range(8):
                                cc = half * 8 + c
                                op("pe", lambda e: e.transpose(out=pbb[:, c * 128:(c + 1) * 128], in_=a_[:, cc * 128:(cc + 1) * 128],
                                                               identity=ident), reads=[a_.r, cb.r], writes=[pb.r], inc=(c == 7))
                            dst = a1T[:, half * 8:(half + 1) * 8, tt * 128:(tt + 1) * 128]
                            srcv = pbb.rearrange("p (c t) -> p c t", t=128)
                            if half == 0:
                                op("act", lambda e: e.activation(out=dst, in_=srcv, func=AF.Copy), reads=[pb.r], writes=[a1T.r])
                            else:
                                op("dve", lambda e: e.tensor_copy(out=dst, in_=srcv), reads=[pb.r], writes=[a1T.r])

                    def p_transposes(tt):
                        pb_ = pbf[tt % 2]
                        op("pool", lambda e: e.tensor_copy(out=pb_[:], in_=pT_[tt][:]), reads=[pT_[tt].r], writes=[pb_.r])
                        pb = bank()
                        pbb = pb[:].bitcast(BF16)
                        for c in range(2):
                            op("pe", lambda e: e.transpose(out=pbb[:, c * 128:(c + 1) * 128], in_=pb_[:, c * 128:(c + 1) * 128],
                                                           identity=ident), reads=[pb_.r, cb.r], writes=[pb.r], inc=(c == 1))
                        op("act", lambda e: e.activation(out=ppT[:, :, tt * 128:(tt + 1) * 128],
                                                         in_=pbb[:, 0:256].rearrange("p (c t) -> p c t", t=128), func=AF.Copy),
                           reads=[pb.r], writes=[ppT.r])

                    for cbk in range(4):
                        wt = wq.pop(0)
                        if cbk + 2 < 4:
                            wq.append(loadw(wsc_out, r_w["out"], cbk + 2))
                        else:
                            wq.append(loadw(wsc_pg, r_w["pg"], cbk - 2))
                        for tt in range(4):
                            b = bank()
                            for kc in range(16):
                                op("pe", lambda e: e.matmul(b[:], lhsT=mx[:, kc, tt * 128:(tt + 1) * 128], rhs=wt[:, kc, :],
                                                            start=(kc == 0), stop=(kc == 15)),
                                   reads=[mx.r, wt.r], writes=[b.r], inc=(kc == 15))
                            hs = hT[tt][:, cbk * 512:(cbk + 1) * 512]
                            op("dve", lambda e: e.tensor_tensor(out=hs, in0=b[:], in1=hs, op=ALU.add),
                               reads=[b.r, hT[tt].r], writes=[hT[tt].r])
                            if cbk == 1:
                                p_transposes(tt)
                            if cbk == 3:
                                norm_chain(tt)
                                if tt >= 1:
                                    transposes(tt - 1)
                    if G + 1 < NG:
                        d_load_mx(G + 1)
                    transposes(3)
                    for cbk in range(4):
                        wt = wq.pop(0)
                        if cbk + 2 < 4:
                            wq.append(loadw(wsc_pg, r_w["pg"], cbk + 2))
                        elif G + 1 < NG:
                            wq.append(loadw(wsc_out, r_w["out"], cbk - 2))
                        cs = slice(cbk * 512, (cbk + 1) * 512)
                        for tt in range(4):
                            bg = bank()
                            for kc in range(16):
                                op("pe", lambda e: e.matmul(bg[:], lhsT=a1T[:, kc, tt * 128:(tt + 1) * 128], rhs=wt[:, kc, :],
                                                            start=(kc == 0), stop=(kc == 15)),
                                   reads=[a1T.r, wt.r], writes=[bg.r], inc=(kc == 15))
                            bp = bank()
                            for kc in range(2):
                                op("pe", lambda e: e.matmul(bp[:], lhsT=ppT[:, kc, tt * 128:(tt + 1) * 128], rhs=wpp[:, kc, cs],
                                                            start=(kc == 0), stop=(kc == 1)),
                                   reads=[ppT.r, wpp.r], writes=[bp.r], inc=(kc == 1))
                            th = nft()
                            op("act", lambda e: e.activation(out=th[:], in_=bg[:], func=AF.Exp, scale=-1.0), reads=[bg.r], writes=[th.r])
                            tl = nft()
                            op("act", lambda e: e.activation(out=tl[:], in_=th[:], func=AF.Ln, bias=onec), reads=[th.r, cf.r], writes=[tl.r])
                            t2 = nft()
                            op("act", lambda e: e.activation(out=t2[:], in_=tl[:], func=AF.Exp, scale=-1.0), reads=[tl.r], writes=[t2.r])
                            t3 = nft()
                            op("dve", lambda e: e.tensor_tensor(out=t3[:], in0=bp[:], in1=t2[:], op=ALU.mult),
                               reads=[bp.r, t2.r], writes=[t3.r])
                            hs = hT[tt][:, cs]
                            op("pool", lambda e: e.tensor_tensor(out=hs, in0=hs, in1=t3[:], op=ALU.add),
                               reads=[hT[tt].r, t3.r], writes=[hT[tt].r])
                    for tt in range(4):
                        r0 = G * 512 + tt * 128
                        dma(h_dst[r0:r0 + 128, :], hT[tt][:], reads=[hT[tt].r], writes=[r_hdst])
                fw.barrier()

        for l in range(depth):
            h_src = x_d if l == 0 else h1_d
            h_dst = y_d if l == depth - 1 else h1_d
            phase_a(l, h_src, r_h[l])
            if stop_after == "A":
                break
            phase_b(l)
            if stop_after == "B":
                break
            phase_c(l)
            if stop_after == "C":
                break
            phase_d(l, h_src, r_h[l], h_dst, r_h[l + 1])

        fw.dead = False
        fw.barrier()
        print("instructions:", fw.ninstr, flush=True)
    return nc


def _consts():
    cbm = np.zeros((128, NCB), np.float32)
    cbm[:, CB_ID:CB_ID + 128] = np.eye(128)
    cbm[:, CB_ONES:CB_ONES + 128] = 1.0

    def perm_block(mat, base, R_):
        half = R_ // 2
        for m in range(R_):
            if m < half:
                mat[base + m + half, base + m] = -1.0
            else:
                mat[base + m - half, base + m] = 1.0

    perm_block(cbm[:, CB_PMLA:CB_PMLA + 128], 0, 64)
    perm_block(cbm[:, CB_PDSA:CB_PDSA + 128], 0, 32)
    perm_block(cbm[:, CB_PIDX:CB_PIDX + 128], 0, 16)
    perm_block(cbm[:, CB_PIDX:CB_PIDX + 128], 64, 16)
    kk = np.arange(128)[:, None]
    qq = np.arange(128)[None, :]
    cbm[:, CB_TRI:CB_TRI + 128] = (kk <= qq).astype(np.float32)
    cfm = np.zeros((128, NCF), np.float32)
    cfm[:, CF_IBIAS:CF_IBIAS + 128] = np.where(qq.T >= kk.T, 0.0, -1e30)
    theta = np.float32(500000.0)
    pidx = np.arange(128)
    inv_mla = np.where(pidx < 64, theta ** (-(np.arange(128) % 32).astype(np.float32) / np.float32(32)), 0.0)
    inv_dsa = np.where(pidx < 32, theta ** (-(np.arange(128) % 16).astype(np.float32) / np.float32(16)), 0.0)
    inv_idx = np.where((pidx % 64) < 16, theta ** (-((np.arange(128) % 64) % 8).astype(np.float32) / np.float32(8)), 0.0)
    cfm[:, CF_INV + 0] = inv_mla
    cfm[:, CF_INV + 1] = inv_dsa
    cfm[:, CF_INV + 2] = inv_idx
    cfm[:, CF_EPS] = EPS
    cfm[:, CF_ONE] = 1.0
    cfm[:, CF_ONESF:CF_ONESF + 128] = 1.0
    return cbm.astype(ml_dtypes.bfloat16), cfm.astype(np.float32)


def _layout_small(inputs, depth):
    sp = np.zeros((depth, 128, NSP), np.float32)
    rowp = np.zeros((depth * 2, 2048), np.float32)
    for l in range(depth):
        sp[l, :, SP_GQ:SP_GQ + 4] = np.asarray(inputs["mla_gq"][l]).reshape(4, 128).T
        sp[l, :, SP_GKV:SP_GKV + 4] = np.asarray(inputs["mla_gkv"][l]).reshape(4, 128).T
        qn = np.asarray(inputs["mla_qn"][l])
        kn = np.asarray(inputs["mla_kn"][l])
        sp[l, :, SP_QN_N] = qn[:128]
        sp[l, :64, SP_QN_R] = qn[128:]
        sp[l, :, SP_KN_N] = kn[:128]
        sp[l, :64, SP_KN_R] = kn[128:]
        sp[l, :, SP_DQN] = np.asarray(inputs["dsa_qn"][l])
        sp[l, :, SP_DKN] = np.asarray(inputs["dsa_kn"][l])
        cw = np.asarray(inputs["conv_w"][l])
        for j in range(4):
            for k in range(3):
                sp[l, :, SP_CONV + j * 3 + k] = cw[k, j * 128:(j + 1) * 128]
        rowp[2 * l] = np.asarray(inputs["norm_in"][l])
        rowp[2 * l + 1] = np.asarray(inputs["ple_norm"][l])
    return sp, rowp


def make_in_maps(inputs, S=4096, depth=2, cores=8):
    cbm, cfm = _consts()
    sp, rowp = _layout_small(inputs, depth)
    f = lambda k: np.ascontiguousarray(np.asarray(inputs[k], dtype=np.float32)[:depth])
    shared = {
        "w_in": f("w_in"), "w_uq": f("mla_w_uq"), "w_ukv": f("mla_w_ukv"), "w_out": f("w_out"),
        "w_pg": f("ple_w_gate"), "w_pp": f("ple_w_proj"), "sp": sp, "rowp": rowp, "cb": cbm, "cf": cfm,
    }
    x = np.asarray(inputs["x"], dtype=np.float32)
    p = np.asarray(inputs["p"], dtype=np.float32)
    pos = np.asarray(inputs["positions"]).astype(np.int32)
    maps = []
    for b in range(cores):
        m = dict(shared)
        m["x"] = np.ascontiguousarray(x[b, :S])
        m["p"] = np.ascontiguousarray(p[:depth, b, :S])
        m["pos"] = np.ascontiguousarray(pos[b:b + 1, :S])
        maps.append(m)
    return maps


def kernel(**inputs):
    nc = build()
    maps = make_in_maps(inputs)
    res = run_bass_kernel_spmd(nc, maps, core_ids=list(range(8)))
    return np.stack([r["y"] for r in res.results], axis=0)
```
